# Optimizing a Trainium2 kernel written in Bass

```python
import math
import numpy as np
import jax, jax.numpy as jnp
from jax import lax

D_MODEL = 1024
BATCH = 8
SEQ = 2048
DEPTH = 2

N_HEADS = 8
HEAD_DIM = 64
N_KV_GROUPS = 2
HEADS_PER_GROUP = N_HEADS // N_KV_GROUPS
ATTN_WIDTH = N_HEADS * HEAD_DIM
KV_WIDTH = N_KV_GROUPS * HEAD_DIM
CMP_BLOCK = 32
CMP_STRIDE = 16
CMP_HIDDEN = 128
SEL_BLOCK = 64
SEL_TOP = 16
WINDOW = 512
Q_CHUNK = 64
FORCE_BONUS = 1.0e4
NEG_INF = -1.0e30
CONV_CH = 512
CONV_WIDTH = 31
FFN_DIM = 2816
FFN_CONV_WIDTH = 3
NORM_EPS = 1e-6

IN_SIZES = (ATTN_WIDTH, KV_WIDTH, KV_WIDTH, KV_WIDTH, KV_WIDTH, KV_WIDTH, KV_WIDTH,
            3 * N_HEADS, 2 * CONV_CH, 2 * D_MODEL)
N_IN = sum(IN_SIZES)

kernel_name = "hybrid_nsa_conformer_convffn"


def rms_norm(x, g):
    xf = x.astype(jnp.float32)
    y = xf * lax.rsqrt(jnp.mean(xf * xf, axis=-1, keepdims=True) + NORM_EPS)
    return (y * g.astype(jnp.float32)).astype(x.dtype)


def layer_norm(x, g, b):
    xf = x.astype(jnp.float32)
    mu = jnp.mean(xf, axis=-1, keepdims=True)
    var = jnp.mean(jnp.square(xf - mu), axis=-1, keepdims=True)
    y = (xf - mu) * lax.rsqrt(var + NORM_EPS)
    return (y * g.astype(jnp.float32) + b.astype(jnp.float32)).astype(x.dtype)


def causal_dwconv(x, w):
    k, c = w.shape
    return lax.conv_general_dilated(x, w[:, None, :], window_strides=(1,), padding=[(k - 1, 0)],
                                    dimension_numbers=('NWC', 'WIO', 'NWC'), feature_group_count=c)


def alibi_slopes():
    return jnp.power(2.0, -8.0 * jnp.arange(1, N_HEADS + 1, dtype=jnp.float32) / N_HEADS)


def compress(kv, pe, w1, w2):
    b, s, g, dh = kv.shape
    ncmp = (s - CMP_BLOCK) // CMP_STRIDE + 1
    idx = np.arange(ncmp)[:, None] * CMP_STRIDE + np.arange(CMP_BLOCK)[None, :]
    blk = kv[:, idx] + pe[None, None, :, None, :]
    blk = jnp.transpose(blk, (0, 1, 3, 2, 4)).reshape(b, ncmp, g, CMP_BLOCK * dh)
    return jax.nn.gelu(blk @ w1) @ w2


def cmp_to_sel_map(s):
    ncmp = (s - CMP_BLOCK) // CMP_STRIDE + 1
    nsel = s // SEL_BLOCK
    cs = np.arange(ncmp) * CMP_STRIDE
    ce = cs + CMP_BLOCK - 1
    ss = np.arange(nsel) * SEL_BLOCK
    se = ss + SEL_BLOCK - 1
    ov = np.minimum(ce[:, None], se[None, :]) - np.maximum(cs[:, None], ss[None, :]) + 1
    return jnp.asarray(np.clip(ov, 0, None).astype(np.float32) / CMP_BLOCK)


def nsa_attention(q, k_cmp, v_cmp, k_sel, v_sel, k_win, v_win, gate_logits, pe_k, pe_v, ck1, ck2, cv1, cv2):
    b, s, _ = q.shape
    G, J, Dh = N_KV_GROUPS, HEADS_PER_GROUP, HEAD_DIM
    f32 = jnp.float32
    q = q.reshape(b, s, G, J, Dh)
    kv = lambda t: t.reshape(b, s, G, Dh)
    k_cmp, v_cmp, k_sel, v_sel, k_win, v_win = map(kv, (k_cmp, v_cmp, k_sel, v_sel, k_win, v_win))
    slopes = alibi_slopes().reshape(G, J)
    pos = jnp.arange(s)
    scale = HEAD_DIM ** -0.5

    kc = compress(k_cmp, pe_k, ck1, ck2)
    vc = compress(v_cmp, pe_v, cv1, cv2)
    ncmp = kc.shape[1]
    c_end = jnp.arange(ncmp) * CMP_STRIDE + CMP_BLOCK - 1
    dist_c = pos[:, None] - c_end[None, :]
    valid_c = dist_c >= 0
    s_c = jnp.einsum('btgjd,bngd->bgjtn', q, kc).astype(f32) * scale - slopes[:, :, None, None] * dist_c
    s_c = jnp.where(valid_c, s_c, NEG_INF)
    p_c = jax.nn.softmax(s_c, axis=-1) * valid_c
    o_c = jnp.einsum('bgjtn,bngd->btgjd', p_c.astype(vc.dtype), vc)

    nsel = s // SEL_BLOCK
    n_top = min(SEL_TOP, nsel)
    imp = jnp.einsum('bgjtn,nk->bgtk', p_c, cmp_to_sel_map(s))
    blk_id = jnp.arange(nsel)[None, :]
    cur = (pos // SEL_BLOCK)[:, None]
    forced = (blk_id == 0) | (blk_id == cur) | (blk_id == cur - 1)
    imp = jnp.where(blk_id <= cur, imp + FORCE_BONUS * forced, NEG_INF)
    _, sel_idx = lax.top_k(imp, n_top)

    ks_blocks = k_sel.reshape(b, nsel, SEL_BLOCK, G, Dh).transpose(0, 3, 1, 2, 4)
    vs_blocks = v_sel.reshape(b, nsel, SEL_BLOCK, G, Dh).transpose(0, 3, 1, 2, 4)
    gather = jax.vmap(jax.vmap(lambda blocks, ids: blocks[ids]))
    kw_pad = jnp.pad(k_win, ((0, 0), (WINDOW, 0), (0, 0), (0, 0)))
    vw_pad = jnp.pad(v_win, ((0, 0), (WINDOW, 0), (0, 0), (0, 0)))
    m_sel = n_top * SEL_BLOCK

    def chunk(c):
        t0 = c * Q_CHUNK
        qc = lax.dynamic_slice_in_dim(q, t0, Q_CHUNK, axis=1)
        tq = t0 + jnp.arange(Q_CHUNK)
        ids = lax.dynamic_slice_in_dim(sel_idx, t0, Q_CHUNK, axis=2)
        ksg = gather(ks_blocks, ids).reshape(b, G, Q_CHUNK, m_sel, Dh)
        vsg = gather(vs_blocks, ids).reshape(b, G, Q_CHUNK, m_sel, Dh)
        kpos = (ids[..., None] * SEL_BLOCK + jnp.arange(SEL_BLOCK)).reshape(b, G, Q_CHUNK, m_sel)
        dist_s = (tq[None, None, :, None] - kpos)[:, :, None]
        s_s = jnp.einsum('btgjd,bgtmd->bgjtm', qc, ksg).astype(f32) * scale \
            - slopes[None, :, :, None, None] * dist_s
        p_s = jax.nn.softmax(jnp.where(dist_s >= 0, s_s, NEG_INF), axis=-1)
        o_s = jnp.einsum('bgjtm,bgtmd->btgjd', p_s.astype(vsg.dtype), vsg)
        kw = lax.dynamic_slice_in_dim(kw_pad, t0, Q_CHUNK + WINDOW, axis=1)
        vw = lax.dynamic_slice_in_dim(vw_pad, t0, Q_CHUNK + WINDOW, axis=1)
        kpos_w = t0 - WINDOW + jnp.arange(Q_CHUNK + WINDOW)
        dist_w = tq[:, None] - kpos_w[None, :]
        valid_w = (dist_w >= 0) & (dist_w < WINDOW) & (kpos_w[None, :] >= 0)
        s_w = jnp.einsum('btgjd,bsgd->bgjts', qc, kw).astype(f32) * scale - slopes[:, :, None, None] * dist_w
        p_w = jax.nn.softmax(jnp.where(valid_w, s_w, NEG_INF), axis=-1)
        o_w = jnp.einsum('bgjts,bsgd->btgjd', p_w.astype(vw.dtype), vw)
        return o_s, o_w

    o_s, o_w = lax.map(chunk, jnp.arange(s // Q_CHUNK))
    o_s = jnp.moveaxis(o_s, 0, 1).reshape(b, s, G, J, Dh)
    o_w = jnp.moveaxis(o_w, 0, 1).reshape(b, s, G, J, Dh)

    g = jax.nn.sigmoid(gate_logits).reshape(b, s, G, J, 3)
    o = g[..., 0:1] * o_c + g[..., 1:2] * o_s + g[..., 2:3] * o_w
    return o.reshape(b, s, ATTN_WIDTH)


def conformer_conv(u, dw_w, dw_b, ln_g, ln_b, w_proj):
    a, gate = jnp.split(u, 2, axis=-1)
    z = a * jax.nn.sigmoid(gate)
    z = causal_dwconv(z, dw_w) + dw_b
    z = jax.nn.silu(layer_norm(z, ln_g, ln_b))
    return z @ w_proj


def conv_ffn(h, w_up, dw_w, dw_b, w_down):
    u = causal_dwconv(h @ w_up, dw_w) + dw_b
    a, v = jnp.split(u, 2, axis=-1)
    return (jax.nn.silu(a) * v) @ w_down


def setup_inputs(seed: int = 0) -> dict:
    key = jax.random.key(seed)
    keys = jax.random.split(key, 32)
    counter = [0]
    f32 = jnp.float32

    def nxt():
        k = keys[counter[0]]
        counter[0] += 1
        return k

    def dense(shape, fan_in):
        return jax.random.normal(nxt(), shape, f32) * fan_in ** -0.5

    def gain(shape):
        return 1.0 + 0.05 * jax.random.normal(nxt(), shape, f32)

    def small(shape, scale=0.02):
        return scale * jax.random.normal(nxt(), shape, f32)

    L = DEPTH
    return {
        "x": jax.random.normal(nxt(), (BATCH, SEQ, D_MODEL), f32),
        "norm1_g": gain((L, D_MODEL)),
        "w_in": dense((L, D_MODEL, N_IN), D_MODEL),
        "cmp_pe_k": small((L, CMP_BLOCK, HEAD_DIM), 0.1),
        "cmp_pe_v": small((L, CMP_BLOCK, HEAD_DIM), 0.1),
        "cmp_k_w1": dense((L, CMP_BLOCK * HEAD_DIM, CMP_HIDDEN), CMP_BLOCK * HEAD_DIM),
        "cmp_k_w2": dense((L, CMP_HIDDEN, HEAD_DIM), CMP_HIDDEN),
        "cmp_v_w1": dense((L, CMP_BLOCK * HEAD_DIM, CMP_HIDDEN), CMP_BLOCK * HEAD_DIM),
        "cmp_v_w2": dense((L, CMP_HIDDEN, HEAD_DIM), CMP_HIDDEN),
        "w_attn_br": dense((L, ATTN_WIDTH, D_MODEL), ATTN_WIDTH),
        "conv_dw_w": dense((L, CONV_WIDTH, CONV_CH), CONV_WIDTH),
        "conv_dw_b": small((L, CONV_CH)),
        "conv_ln_g": gain((L, CONV_CH)),
        "conv_ln_b": small((L, CONV_CH)),
        "w_conv_br": dense((L, CONV_CH, D_MODEL), CONV_CH),
        "w_out": dense((L, D_MODEL, D_MODEL), D_MODEL),
        "norm2_g": gain((L, D_MODEL)),
        "ffn_w_up": dense((L, D_MODEL, 2 * FFN_DIM), D_MODEL),
        "ffn_dw_w": dense((L, FFN_CONV_WIDTH, 2 * FFN_DIM), FFN_CONV_WIDTH),
        "ffn_dw_b": small((L, 2 * FFN_DIM)),
        "ffn_w_down": dense((L, FFN_DIM, D_MODEL), FFN_DIM),
        "final_g": gain((D_MODEL,)),
    }


def reference(x, norm1_g, w_in, cmp_pe_k, cmp_pe_v, cmp_k_w1, cmp_k_w2, cmp_v_w1, cmp_v_w2,
              w_attn_br, conv_dw_w, conv_dw_b, conv_ln_g, conv_ln_b, w_conv_br, w_out,
              norm2_g, ffn_w_up, ffn_dw_w, ffn_dw_b, ffn_w_down, final_g):
    split_points = list(np.cumsum(IN_SIZES)[:-1])
    for l in range(DEPTH):
        h = rms_norm(x, norm1_g[l])
        proj = h @ w_in[l]
        (q, k_c, v_c, k_s, v_s, k_w, v_w, g_nsa, u_conv, g_merge) = jnp.split(proj, split_points, axis=-1)
        y_attn = nsa_attention(q, k_c, v_c, k_s, v_s, k_w, v_w, g_nsa, cmp_pe_k[l], cmp_pe_v[l],
                               cmp_k_w1[l], cmp_k_w2[l], cmp_v_w1[l], cmp_v_w2[l]) @ w_attn_br[l]
        y_conv = conformer_conv(u_conv, conv_dw_w[l], conv_dw_b[l], conv_ln_g[l], conv_ln_b[l], w_conv_br[l])
        g_a, g_b = jnp.split(jax.nn.sigmoid(g_merge), 2, axis=-1)
        x = x + (g_a * y_attn + g_b * y_conv) @ w_out[l]
        x = x + conv_ffn(rms_norm(x, norm2_g[l]), ffn_w_up[l], ffn_dw_w[l], ffn_dw_b[l], ffn_w_down[l])
    return rms_norm(x, final_g)
```

```python
import numpy as np
import ml_dtypes
from contextlib import ExitStack
import concourse.bass as bass
import concourse.mybir as mybir
from concourse.bass_utils import run_bass_kernel_spmd

F32 = mybir.dt.float32
BF16 = mybir.dt.bfloat16
AF = mybir.ActivationFunctionType
ALU = mybir.AluOpType

S = 2048
D = 1024
L = 2
NH = 8
FFN = 2816
EPS = 1e-6
BIG = 32768.0
ENGS = ("pe", "act", "dve", "pool", "sp")
ATTACH_WAITS = True

UNITS = {}
_off = 0


def _u(name, e):
    global _off
    UNITS[name] = (_off, e)
    _off += e


for _i in range(4):
    _u(("FM", _i), 2048)
_u("TOKA", 2048)
_u("TOKB", 192)
for _i in range(4):
    _u(("UC", _i), 2048)
for _i in range(8):
    _u(("D1a", _i), 2048)
    _u(("D1b", _i), 1024)
for _i in range(8):
    _u(("WO", _i), 1024)
for _i in range(22):
    _u(("UP", _i), 2048)
for _i in range(8):
    for _h in range(2):
        _u(("DN", _i, _h), 1408)
for _kv in range(2):
    for _h in range(2):
        _u(("CW", _kv, _h), 2048)
_u("W2", 128)
TOT = _off

NPRM_L = 392
NPRM = 2 * NPRM_L + 8

CB_IDENT = 0
CB_ONES = 128
CB_CAUS = 256
CB_EDGE = 384
CB_CMPNEG = 512
CB_KSEL = 2560
CB_KWIN = 4608
CB_KCMP = 6656
CB_MAP = 6784
CB_QAUG = 6848
NCB = CB_QAUG + 8 * 2048


class _Rec:
    def __init__(self):
        self.call = None

    def __getattr__(self, name):
        def f(*a, **k):
            self.call = (name, a, k)
            return None
        return f


def _bind(fn):
    rec = _Rec()
    fn(rec)
    name, a, k = rec.call
    return lambda eng: getattr(eng, name)(*a, **k)


class Prog:
    def __init__(self, nc):
        self.nc = nc
        self.ops = {e: [] for e in ENGS}
        self.cnt = {e: 0 for e in ENGS}
        self.seen = {e: {} for e in ENGS}
        self.cells = {}
        self.streams = {}

    def _deps(self, reads, writes):
        need = {}

        def add(tok):
            if tok is None:
                return
            k, v = tok
            if need.get(k, 0) < v:
                need[k] = v

        for c in reads:
            st = self.cells.get(c)
            if st is not None:
                add(st[0])
        for c in writes:
            st = self.cells.get(c)
            if st is not None:
                add(st[0])
                for t in st[1]:
                    add(t)
        return need

    def _commit(self, tok, reads, writes):
        for c in reads:
            st = self.cells.get(c)
            if st is None:
                st = [None, []]
                self.cells[c] = st
            st[1].append(tok)
            if len(st[1]) > 16:
                m = {}
                for k, v in st[1]:
                    if m.get(k, 0) < v:
                        m[k] = v
                st[1] = list(m.items())
        for c in writes:
            self.cells[c] = [tok, []]

    def _waits(self, eng, need):
        waits = []
        seen = self.seen[eng]
        for k, v in need.items():
            if k == eng and eng == "pe":
                continue
            if seen.get(k, 0) >= v:
                continue
            seen[k] = v
            waits.append((k, v))
        return waits

    def op(self, eng, fn, reads=(), writes=(), inc=True):
        need = self._deps(reads, writes)
        waits = self._waits(eng, need)
        tok = (eng, self.cnt[eng] + 1)
        if inc:
            self.cnt[eng] += 1
        self.ops[eng].append((waits, _bind(fn), "eng" if inc else None, None))
        self._commit(tok, reads, writes)

    def dma(self, qeng, stream, fn, reads=(), writes=()):
        self.streams[stream] = self.streams.get(stream, 0) + 1
        need = self._deps(reads, writes)
        waits = self._waits(qeng, need)
        tok = (("dma", stream), 16 * self.streams[stream])
        self.ops[qeng].append((waits, _bind(fn), "dma", stream))
        self._commit(tok, reads, writes)

    def barrier(self):
        need = {e: self.cnt[e] for e in ENGS if self.cnt[e] > 0}
        for s, c in self.streams.items():
            need[("dma", s)] = 16 * c
        for e in ENGS:
            w = self._waits(e, dict(need))
            if w:
                self.ops[e].append((w, None, None, None))

    def emit(self):
        nc = self.nc
        with ExitStack() as es:
            sems = {}
            for e in ENGS:
                sems[e] = es.enter_context(nc.semaphore("s_" + e))
            for i, s in enumerate(self.streams):
                sems[("dma", s)] = es.enter_context(nc.semaphore("d%d" % i))
            block = es.enter_context(nc.Block())

            def run(eng_name):
                def body(eng):
                    for waits, fn, kind, stream in self.ops[eng_name]:
                        if fn is None:
                            for k, v in waits:
                                eng.wait_ge(sems[k], v)
                            continue
                        attach = None
                        if waits and ATTACH_WAITS and kind != "dma":
                            attach = waits[-1]
                            waits = waits[:-1]
                        for k, v in waits:
                            eng.wait_ge(sems[k], v)
                        ins = fn(eng)
                        if attach is not None:
                            ins._wait_ge(sems[attach[0]], attach[1])
                        if kind == "eng":
                            ins.then_inc(sems[eng_name], 1)
                        elif kind == "dma":
                            ins.then_inc(sems[("dma", stream)], 16)
                return body

            block.tensor(run("pe"))
            block.scalar(run("act"))
            block.vector(run("dve"))
            block.gpsimd(run("pool"))
            block.sync(run("sp"))


def build_program(debug=False, nlayers=L, stop_after=None):
    nc = bass.Bass("TRN2", target_bir_lowering=False)
    x_d = nc.dram_tensor("x", [S, D], F32, kind="ExternalInput").ap()
    wpk_d = nc.dram_tensor("wpk", [L, 128, TOT], F32, kind="ExternalInput").ap()
    prm_d = nc.dram_tensor("prm", [128, NPRM], F32, kind="ExternalInput").ap()
    cf_d = nc.dram_tensor("cf", [128, 640], F32, kind="ExternalInput").ap()
    cb_d = nc.dram_tensor("cb", [128, NCB], BF16, kind="ExternalInput").ap()
    out_d = nc.dram_tensor("out", [S, D], F32, kind="ExternalOutput").ap()
    dbg = {}
    if debug:
        for nm, shp in (("d_hT", [128, 8 * S]), ("d_o", [128, 4 * S]), ("d_s", [128, 4 * S]),
                        ("d_xmix", [128, 8 * S]), ("d_xffn", [128, 8 * S]), ("d_q", [128, 8 * S]),
                        ("d_kc", [128, 256]), ("d_vc", [128, 194]), ("d_imp", [128, 16 * 64]),
                        ("d_sel", [128, 16 * 64]), ("d_zc", [128, 4 * S])):
            dbg[nm] = nc.dram_tensor(nm, shp, F32, kind="ExternalOutput").ap()

    es = ExitStack()
    P = Prog(nc)

    def sb(name, shape, dt):
        return es.enter_context(nc.sbuf_tensor(name, shape, dt))

    XT_B = 0
    R1_B = 65536
    R3_B = R1_B + 32768
    R2_B = R3_B + 33024
    WST_B = R2_B + 32768
    WBF_B = WST_B + 16384
    ARENA_B = WBF_B + 8192
    arena = sb("arena", [128, ARENA_B // 4], F32)

    arena_bf = arena[:].bitcast(BF16)

    def view(off_b, nbytes, dt, pattern=None, **kw):
        if dt == BF16:
            ap = arena_bf[:, off_b // 2:(off_b + nbytes) // 2]
        else:
            ap = arena[:, off_b // 4:(off_b + nbytes) // 4]
        if pattern is not None:
            ap = ap.rearrange(pattern, **kw)
        return ap

    xT = view(XT_B, 65536, F32, "p (c t) -> p c t", c=8)
    hT_A = view(R1_B, 32768, BF16, "p (c t) -> p c t", c=8)
    oT = view(R1_B, 16384, BF16, "p (c t) -> p c t", c=4)
    o_tok = view(R1_B + 16384, 8192, F32, "p (a d) -> p a d", a=4)
    pT = [view(R1_B + 24576 + 1024 * i, 1024, BF16) for i in range(4)]
    sT = view(R1_B + 16384, 16384, BF16, "p (c t) -> p c t", c=4)
    qA = view(R2_B, 32768, BF16, "p (h t) -> p h t", h=8)
    hT_C = view(R2_B, 32768, BF16, "p (c t) -> p c t", c=8)
    kA_sel = view(R3_B, 8192, BF16, "p (g t) -> p g t", g=2)
    kA_win = view(R3_B + 8192, 8192, BF16, "p (g t) -> p g t", g=2)
    kcT = view(R3_B + 16384, 8192, BF16, "p (k t) -> p k t", k=2)
    vtok = view(R3_B + 24576, 8320, BF16, "p (t b g d) -> p t b g d", t=16, b=2, g=2)
    ZW = 30 + S
    zT = view(R3_B, 4 * ZW * 2, BF16, "p (c t) -> p c t", c=4)
    zc_R3 = view(R3_B + 16640, 8192, F32, "p (c t) -> p c t", c=4)
    lnt = view(R3_B + 16640 + 8192, 6144, F32, "p (a t) -> p a t", a=3)
    mT = view(R3_B, 32768, BF16, "p (c t) -> p c t", c=8)
    gT = view(R1_B, 45056, BF16, "p (c t) -> p c t", c=22)
    EB = R1_B + 45056
    esil = [view(EB + p_ * 2048, 2048, F32) for p_ in range(2)]
    eacc = [view(EB + 4096 + p_ * 2048, 2048, F32) for p_ in range(2)]
    eub = [view(EB + 8192 + p_ * 1056, 1028, BF16) for p_ in range(2)]
    edg = [view(EB + 8192 + 2112 + p_ * 768, 768, BF16, "p (k n) -> p k n", k=3) for p_ in range(2)]
    wst = [view(WST_B + 8192 * i, 8192, F32) for i in range(2)]
    wbf = [view(WBF_B + 4096 * i, 4096, BF16) for i in range(2)]

    identf = sb("identf", [128, 128], F32)
    SMA = sb("sma", [128, 2688], F32)

    SMA_bf = SMA[:].bitcast(BF16)

    def smv(off_b, nbytes, dt, pattern=None, **kw):
        if dt == BF16:
            ap = SMA_bf[:, off_b // 2:(off_b + nbytes) // 2]
        else:
            ap = SMA[:, off_b // 4:(off_b + nbytes) // 4]
        if pattern is not None:
            ap = ap.rearrange(pattern, **kw)
        return ap

    bonus = smv(0, 2048, F32, "p (a b) -> p a b", a=16)
    prm = sb("prm_sb", [128, NPRM], F32)
    cbs = sb("cbs", [128, CB_CMPNEG], BF16)
    identb = cbs[:, CB_IDENT:CB_IDENT + 128]
    onesb = cbs[:, CB_ONES:CB_ONES + 128]
    causb = cbs[:, CB_CAUS:CB_CAUS + 128]
    edgeb = cbs[:, CB_EDGE:CB_EDGE + 128]
    cmpneg = smv(2048, 4096, BF16)
    vcA = sb("vcA", [128, 2, 97], BF16)
    kA_cmp = sb("kA_cmp", [128, 2, 128], BF16)
    gates = smv(6144, 1536, F32, "p (t n) -> p t n", t=16)
    impacc = smv(7680, 1024, F32, "p (a g k) -> p a g k", a=4, g=2)
    impf = sb("impf", [128, 32], F32)
    scr32 = sb("scr32", [128, 32], F32)
    m8 = sb("m8", [128, 8], F32)
    Maug = sb("Maug", [128, 4, 32], BF16)
    r4 = sb("r4", [128, 4], F32)
    gr4 = sb("gr4", [128, 4], F32)
    rstd = sb("rstd", [128, 512], F32)
    sqb = [sb("sqb%d" % i, [128, 512], BF16) for i in range(2)]
    peT = sb("peT", [128, 64], BF16)
    cbias = sb("cbias", [128, 2], F32)
    hu = sb("hu", [128, 128], F32)
    ht1 = sb("ht1", [128, 128], F32)
    ht2 = sb("ht2", [128, 128], F32)
    hidb = sb("hidb", [128, 128], BF16)
    w2s = sb("w2s", [128, 128], BF16)
    ecrA = sb("ecrA", [128, 2], F32)
    ecrB = sb("ecrB", [128, 2], F32)
    ubuf = [smv(2056 * i, 2056, F32) for i in range(2)]
    facc = [smv(4112 + 2048 * i, 2048, F32) for i in range(2)]
    fsil = smv(8208, 2048, F32)
    ytok = [view(WST_B + 4096 * i, 4096, F32) for i in range(2)]

    banks = [es.enter_context(nc.psum_tensor("bank%d" % i, [128, 512], F32)) for i in range(8)]

    def bk(i):
        return ("ps", i)

    P.dma("sp", "c0", lambda e: e.dma_start(out=identf[:], in_=cf_d[:, 0:128]), writes=["identf"])
    P.dma("sp", "c2", lambda e: e.dma_start(out=prm[:], in_=prm_d[:, :]), writes=["prm"])
    P.dma("sp", "c3", lambda e: e.dma_start(out=cbs[:], in_=cb_d[:, 0:CB_CMPNEG]), writes=["cbs"])
    P.dma("sp", "c4", lambda e: e.dma_start(out=vcA[:, 0, 64:97], in_=cb_d[:, CB_MAP:CB_MAP + 33]), writes=["vcA_c"])
    P.dma("sp", "c4", lambda e: e.dma_start(out=vcA[:, 1, 64:97], in_=cb_d[:, CB_MAP:CB_MAP + 33]), writes=["vcA_c"])

    P.op("pool", lambda e: e.memset(vcA[:, :, 0:64], 0.0), writes=[("vcA", 0), ("vcA", 1)])

    xin = [view(R1_B + 4096 * i, 4096, F32) for i in range(2)]
    for tt in range(16):
        b = tt % 2
        P.dma("sp", "xin%d" % b, lambda e, tt=tt, b=b: e.dma_start(out=xin[b], in_=x_d[tt * 128:(tt + 1) * 128, :]),
              writes=[("xin", b)])
        for half in range(2):
            pb = (tt * 2 + half) % 4
            for j in range(4):
                c = half * 4 + j
                P.op("pe", lambda e, b=b, c=c, j=j, pb=pb: e.transpose(out=banks[pb][:, j * 128:(j + 1) * 128], in_=xin[b][:, c * 128:(c + 1) * 128], identity=identf[:]),
                     reads=[("xin", b), "identf"], writes=[bk(pb)], inc=(j == 3))
            eng = "act" if half == 0 else "dve"
            if eng == "act":
                P.op("act", lambda e, tt=tt, half=half, pb=pb: e.activation(out=xT[:, half * 4:half * 4 + 4, tt * 128:(tt + 1) * 128],
                                                                          in_=banks[pb][:].rearrange("p (j t) -> p j t", j=4), func=AF.Copy),
                     writes=[bk(pb)] + [("xT", half * 4 + j, tt // 4) for j in range(4)])
            else:
                P.op("dve", lambda e, tt=tt, half=half, pb=pb: e.tensor_copy(out=xT[:, half * 4:half * 4 + 4, tt * 128:(tt + 1) * 128],
                                                                           in_=banks[pb][:].rearrange("p (j t) -> p j t", j=4)),
                     writes=[bk(pb)] + [("xT", half * 4 + j, tt // 4) for j in range(4)])
    P.barrier()

    wctr = [0]

    wst.append(view(EB + 12288, 8192, F32))
    assert 8192 + 2112 + 1536 <= 12288
    wbf.append(smv(0, 4096, BF16))

    def wload(l, name, shape_pat=None, cast="pool", slot=None, **kw):
        off, e = UNITS[name]
        if slot is None:
            slot = wctr[0] % 2
            wctr[0] += 1
        P.dma("sp", "w%d" % slot, lambda en: en.dma_start(out=wst[slot][:, 0:e], in_=wpk_d[l, :, off:off + e]),
              writes=[("wst", slot)])
        if cast == "act":
            P.op("act", lambda en: en.activation(out=wbf[slot][:, 0:e], in_=wst[slot][:, 0:e], func=AF.Copy),
                 reads=[("wst", slot)], writes=[("wbf", slot)])
        else:
            P.op(cast, lambda en: en.tensor_copy(out=wbf[slot][:, 0:e], in_=wst[slot][:, 0:e]),
                 reads=[("wst", slot)], writes=[("wbf", slot)])
        ap = wbf[slot][:, 0:e]
        if shape_pat is not None:
            ap = ap.rearrange(shape_pat, **kw)
        return ap, ("wbf", slot)

    def wdma(l, name, stage):
        off, e = UNITS[name]
        P.dma("sp", "w%d" % stage, lambda en: en.dma_start(out=wst[stage][:, 0:e], in_=wpk_d[l, :, off:off + e]), writes=[("wst", stage)])

    def wcast(name, stage, bf, bfcell, cast):
        off, e = UNITS[name]
        if cast == "act":
            P.op("act", lambda en: en.activation(out=bf[:, 0:e], in_=wst[stage][:, 0:e], func=AF.Copy), reads=[("wst", stage)], writes=[bfcell])
        else:
            P.op(cast, lambda en: en.tensor_copy(out=bf[:, 0:e], in_=wst[stage][:, 0:e]), reads=[("wst", stage)], writes=[bfcell])

    class WStream:
        def __init__(self, l, specs, nslots=2, dist=1):
            self.l, self.specs, self.n, self.dist = l, specs, nslots, dist
            self.nxt = 0
            self.aps = {}

        def get(self, i):
            hi = min(len(self.specs) - 1, i + self.dist)
            while self.nxt <= hi:
                name, pat, kw, cast = self.specs[self.nxt]
                self.aps[self.nxt] = wload(self.l, name, pat, cast=cast, slot=self.nxt % self.n, **kw)
                self.nxt += 1
            return self.aps.pop(i)

    def rmsnorm(hT, gcol, hname, sbank=6):
        for tg in range(4):
            ts = slice(tg * 512, (tg + 1) * 512)
            for c in range(8):
                q = c % 2
                P.op("act", lambda e, c=c, q=q: e.activation(out=sqb[q][:], in_=xT[:, c, ts], func=AF.Square),
                     reads=[("xT", c, tg)], writes=[("sqb", q)])
                P.op("pe", lambda e, c=c, q=q: e.matmul(out=banks[sbank][:], lhsT=onesb, rhs=sqb[q][:], start=(c == 0), stop=(c == 7)),
                     reads=[("sqb", q), "cbs"], writes=[bk(sbank)], inc=True)
            P.op("act", lambda e: e.activation(out=rstd[:], in_=banks[sbank][:], func=AF.Sqrt, bias=EPS, scale=1.0 / D),
                 writes=[bk(sbank), "rstd"])
            P.op("dve", lambda e: e.reciprocal(out=rstd[:], in_=rstd[:]), writes=["rstd"])
            for c in range(8):
                P.op("dve", lambda e, c=c: e.scalar_tensor_tensor(out=hT[:, c, ts], in0=xT[:, c, ts], scalar=prm[:, gcol + c:gcol + c + 1],
                                                                  in1=rstd[:], op0=ALU.mult, op1=ALU.mult),
                     reads=[("xT", c, tg), "rstd", "prm"], writes=[(hname, c, tg)])

    def dump(name, ap_src, cols, cells):
        if debug and name in dbg:
            P.barrier()
            P.dma("sp", "dbg_" + name, lambda e: e.dma_start(out=dbg[name][:, 0:cols], in_=ap_src), reads=cells)
            P.barrier()

    slopes = [2.0 ** (-(h + 1)) for h in range(NH)]

    for l in range(nlayers):
        pb0 = l * NPRM_L
        rmsnorm(hT_A, pb0 + 0, "hT")
        if debug and l == 0:
            pass
        P.dma("sp", "c1", lambda e: e.dma_start(out=bonus.rearrange("p a b -> p (a b)"), in_=cf_d[:, 128:640]), writes=["bonus"])
        P.dma("sp", "c5", lambda e: e.dma_start(out=cmpneg, in_=cb_d[:, CB_CMPNEG:CB_CMPNEG + 2048]), writes=["cmpneg"])
        P.dma("sp", "aq", lambda e: e.dma_start(out=qA[0:64, :, :].rearrange("p h t -> p (h t)"), in_=cb_d[0:64, CB_QAUG:CB_QAUG + 8 * S]), writes=["qA_aug"])
        for g in range(2):
            P.dma("sp", "ak", lambda e, g=g: e.dma_start(out=kA_sel[0:64, g, :], in_=cb_d[0:64, CB_KSEL:CB_KSEL + S]), writes=[("kA_aug", 0, g)])
            P.dma("sp", "ak", lambda e, g=g: e.dma_start(out=kA_win[0:64, g, :], in_=cb_d[0:64, CB_KWIN:CB_KWIN + S]), writes=[("kA_aug", 1, g)])
            P.dma("sp", "ak", lambda e, g=g: e.dma_start(out=kA_cmp[0:64, g, :], in_=cb_d[0:64, CB_KCMP:CB_KCMP + 128]), writes=[("kA_aug", 2, g)])
        P.op("pool", lambda e: e.memset(kA_cmp[64:128, :, :], 0.0), writes=["kA_cmp_d"])
        P.op("pool", lambda e: e.memset(vtok[:, :, :, :, 64:65], 1.0), writes=["vtok_ones"])

        pbr = 0
        wsA = WStream(l, [(("FM", u), "p (k n) -> p k n", dict(k=8), "pool") for u in range(4)]
                      + [("TOKA", "p (k n) -> p k n", dict(k=8), "pool"), ("TOKB", "p (k n) -> p k n", dict(k=8), "pool")])
        for u in range(4):
            w, wc = wsA.get(u)
            for tg in range(4):
                ts = slice(tg * 512, (tg + 1) * 512)
                for ci in range(2):
                    ch = u * 2 + ci
                    pb = pbr % 4
                    pbr += 1
                    for k in range(8):
                        P.op("pe", lambda e, k=k, ci=ci, pb=pb, w=w: e.matmul(out=banks[pb][:], lhsT=w[:, k, ci * 128:(ci + 1) * 128], rhs=hT_A[:, k, ts],
                                                                             start=(k == 0), stop=(k == 7)),
                             reads=[wc, ("hT", k, tg)], writes=[bk(pb)], inc=(k == 7))
                    if ch < 4:
                        P.op("dve", lambda e, ch=ch, pb=pb: e.tensor_copy(out=qA[64:128, 2 * ch, ts], in_=banks[pb][0:64, :]),
                             writes=[bk(pb), ("qA", 2 * ch, tg)])
                        P.op("act", lambda e, ch=ch, pb=pb: e.activation(out=qA[64:128, 2 * ch + 1, ts], in_=banks[pb][64:128, :], func=AF.Copy),
                             writes=[bk(pb), ("qA", 2 * ch + 1, tg)])
                    elif ch in (4, 5):
                        P.op("act", lambda e, ch=ch, pb=pb: e.activation(out=kcT[:, ch - 4, ts], in_=banks[pb][:], func=AF.Copy),
                             writes=[bk(pb), ("kcT", ch - 4, tg)])
                    else:
                        dst = kA_sel if ch == 6 else kA_win
                        nm = "kS" if ch == 6 else "kW"
                        P.op("dve", lambda e, dst=dst, pb=pb: e.tensor_copy(out=dst[64:128, 0, ts], in_=banks[pb][0:64, :]),
                             writes=[bk(pb), (nm, 0, tg)])
                        P.op("act", lambda e, dst=dst, pb=pb: e.activation(out=dst[64:128, 1, ts], in_=banks[pb][64:128, :], func=AF.Copy),
                             writes=[bk(pb), (nm, 1, tg)])
        w, wc = wsA.get(4)
        for tt in range(16):
            pb = 4 + tt % 2
            for k in range(8):
                P.op("pe", lambda e, k=k, tt=tt, pb=pb, w=w: e.matmul(out=banks[pb][:, 0:256], lhsT=hT_A[:, k, tt * 128:(tt + 1) * 128], rhs=w[:, k, :],
                                                                     start=(k == 0), stop=(k == 7)),
                     reads=[wc, ("hT", k, tt // 4)], writes=[bk(pb)], inc=(k == 7))
            eng = "act" if tt % 2 == 0 else "dve"
            if eng == "act":
                P.op("act", lambda e, tt=tt, pb=pb: e.activation(out=vtok[:, tt, :, :, 0:64].rearrange("p b g d -> p (b g) d"),
                                                               in_=banks[pb][:, 0:256].rearrange("p (a d) -> p a d", d=64), func=AF.Copy),
                     reads=["vtok_ones"], writes=[bk(pb), ("vtok", tt)])
            else:
                P.op("dve", lambda e, tt=tt, pb=pb: e.tensor_copy(out=vtok[:, tt, :, :, 0:64].rearrange("p b g d -> p (b g) d"),
                                                                in_=banks[pb][:, 0:256].rearrange("p (a d) -> p a d", d=64)),
                     reads=["vtok_ones"], writes=[bk(pb), ("vtok", tt)])
        w, wc = wsA.get(5)
        for tt in range(16):
            pb = 4 + tt % 2
            for k in range(8):
                P.op("pe", lambda e, k=k, tt=tt, pb=pb, w=w: e.matmul(out=banks[pb][:, 0:24], lhsT=hT_A[:, k, tt * 128:(tt + 1) * 128], rhs=w[:, k, :],
                                                                     start=(k == 0), stop=(k == 7)),
                     reads=[wc, ("hT", k, tt // 4)], writes=[bk(pb)], inc=(k == 7))
            P.op("act", lambda e, tt=tt, pb=pb: e.activation(out=gates[:, tt, :], in_=banks[pb][:, 0:24], func=AF.Sigmoid),
                 writes=[bk(pb), ("gates", tt)])

        P.op("dve", lambda e: e.tensor_copy(out=peT[:], in_=prm[:, pb0 + 328:pb0 + 392]), reads=["prm"], writes=["peT"])
        w2t, w2tc = wload(l, "W2")
        P.op("dve", lambda e: e.tensor_copy(out=w2s[:], in_=w2t), reads=[w2tc], writes=["w2s"])
        w2, w2c = w2s, "w2s"
        kc16 = kcT[:, :, :].rearrange("p k (n s) -> p k n s", s=16)
        for kv in range(2):
            wh = []
            for half in range(2):
                wh.append(wload(l, ("CW", kv, half), "p (a c) -> p a c", cast=("act" if half == 0 else "pool"), a=16))
            for li in range(32):
                w, wc = wh[li // 16]
                P.op("pe", lambda e, li=li, w=w, kv=kv: e.matmul(out=banks[7][:, kv:kv + 1], lhsT=w[0:64, li % 16, :], rhs=peT[0:64, kv * 32 + li:kv * 32 + li + 1],
                                                                start=(li == 0), stop=(li == 31)),
                     reads=[wc, "peT"], writes=[bk(7)], inc=(li == 31))
            P.op("dve", lambda e, kv=kv: e.tensor_copy(out=cbias[:, kv:kv + 1], in_=banks[7][:, kv:kv + 1]), writes=[bk(7), ("cbias", kv)])
            for g in range(2):
                rows = slice(g * 64, (g + 1) * 64)
                for li in range(32):
                    w, wc = wh[li // 16]
                    if li < 16:
                        rhs = kc16[rows, kv, 0:127, li]
                    else:
                        rhs = kc16[rows, kv, 1:128, li - 16]
                    P.op("pe", lambda e, li=li, w=w, rhs=rhs, rows=rows: e.matmul(out=banks[5][:, 0:127], lhsT=w[rows, li % 16, :], rhs=rhs,
                                                                                 start=(li == 0), stop=(li == 31)),
                         reads=[wc] + [("kcT", kv, tg) for tg in range(4)], writes=[bk(5)], inc=(li == 31))
                P.op("act", lambda e, kv=kv: e.activation(out=hu[:, 0:127], in_=banks[5][:, 0:127], func=AF.Identity, bias=cbias[:, kv:kv + 1], scale=1.0),
                     reads=[("cbias", kv)], writes=[bk(5), "hu"])
                P.op("dve", lambda e: e.tensor_tensor(out=ht1[:, 0:127], in0=hu[:, 0:127], in1=hu[:, 0:127], op=ALU.mult), reads=["hu"], writes=["ht1"])
                P.op("dve", lambda e: e.tensor_scalar(out=ht1[:, 0:127], in0=ht1[:, 0:127], scalar1=0.044715, scalar2=1.0, op0=ALU.mult, op1=ALU.add), writes=["ht1"])
                P.op("dve", lambda e: e.tensor_tensor(out=ht2[:, 0:127], in0=ht1[:, 0:127], in1=hu[:, 0:127], op=ALU.mult), reads=["ht1", "hu"], writes=["ht2"])
                P.op("act", lambda e: e.activation(out=ht1[:, 0:127], in_=ht2[:, 0:127], func=AF.Sigmoid, scale=1.5957691216057308), reads=["ht2"], writes=["ht1"])
                P.op("dve", lambda e: e.tensor_tensor(out=hidb[:, 0:127], in0=ht1[:, 0:127], in1=hu[:, 0:127], op=ALU.mult), reads=["ht1", "hu"], writes=["hidb"])
                if kv == 0:
                    P.op("pe", lambda e, w2=w2: e.matmul(out=banks[6][0:64, 0:127], lhsT=w2[:, 0:64], rhs=hidb[:, 0:127], start=True, stop=True),
                         reads=[w2c, "hidb"], writes=[bk(6)], inc=True)
                    P.op("dve", lambda e, g=g: e.tensor_copy(out=kA_cmp[64:128, g, 0:127], in_=banks[6][0:64, 0:127]),
                         reads=["kA_cmp_d"], writes=[bk(6), ("kA_cmp", g)])
                else:
                    P.op("pe", lambda e, w2=w2: e.matmul(out=banks[6][0:127, 0:64], lhsT=hidb[:, 0:127], rhs=w2[:, 64:128], start=True, stop=True),
                         reads=[w2c, "hidb"], writes=[bk(6)], inc=True)
                    P.op("dve", lambda e, g=g: e.tensor_copy(out=vcA[0:127, g, 0:64], in_=banks[6][0:127, 0:64]),
                         writes=[bk(6), ("vcA", g)])
        P.barrier()
        if debug and l == 0:
            P.op("dve", lambda e: e.tensor_copy(out=ytok[0][:, 0:256], in_=kA_cmp[:].rearrange("p g n -> p (g n)")), writes=["ytok0"])
            dump("d_kc", ytok[0][:, 0:256], 256, ["ytok0"])
            P.op("dve", lambda e: e.tensor_copy(out=ytok[0][:, 0:194], in_=vcA[:].rearrange("p g n -> p (g n)")), writes=["ytok0"])
            dump("d_vc", ytok[0][:, 0:194], 194, ["ytok0"])
            for i4 in range(4):
                P.op("dve", lambda e, i4=i4: e.tensor_copy(out=ytok[0], in_=kcT[:, i4 // 2, (i4 % 2) * 1024:(i4 % 2 + 1) * 1024]), writes=["ytok0"])
                P.barrier()
                P.dma("sp", "dbg_q", lambda e, i4=i4: e.dma_start(out=dbg["d_q"][:, i4 * 1024:(i4 + 1) * 1024], in_=ytok[0]), reads=["ytok0"])
                P.barrier()
            P.op("dve", lambda e: e.tensor_copy(out=ytok[0][:, 0:128], in_=hu[:]), writes=["ytok0"])
            P.op("dve", lambda e: e.tensor_copy(out=ytok[0][:, 128:130], in_=cbias[:]), writes=["ytok0"])
            P.op("dve", lambda e: e.tensor_copy(out=ytok[0][:, 130:258], in_=hidb[:]), writes=["ytok0"])
            P.op("dve", lambda e: e.tensor_copy(out=ytok[0][:, 258:322], in_=peT[:]), writes=["ytok0"])
            dump("d_hT", ytok[0][:, 0:322], 322, ["ytok0"])
        if stop_after == "A":
            break

        sctr = [0]
        pctr = [0]
        otr = [0]
        bctr = [0]
        LOOK = 3

        def emit_qk(it):
            blk = it["blk"]
            h, g, br, tg = blk["h"], blk["g"], blk["br"], blk["tg"]
            kt, c0, c1 = it["kt"], it["c0"], it["c1"]
            sbk = sctr[0] % 3
            sctr[0] += 1
            if br == 0:
                lhs = kA_cmp[:, g, :]
                kcell = [("kA_cmp", g), ("kA_aug", 2, g)]
            elif br == 1:
                lhs = kA_sel[:, g, kt * 128:(kt + 1) * 128]
                kcell = [("kS", g, kt // 4), ("kA_aug", 0, g)]
            else:
                lhs = kA_win[:, g, kt * 128:(kt + 1) * 128]
                kcell = [("kW", g, kt // 4), ("kA_aug", 1, g)]
            masks = []
            if br == 0:
                masks.append((0, 512, cmpneg[:, tg * 512:(tg + 1) * 512]))
            else:
                dcol = 128 * kt - 512 * tg
                if 0 <= dcol < 512:
                    masks.append((dcol, dcol + 128, causb))
                if br == 2:
                    ecol = 128 * (kt + 4) - 512 * tg
                    if 0 <= ecol < 512:
                        masks.append((ecol, ecol + 128, edgeb))
            qcells = [("qA", h, tg), "qA_aug"] + ([("qsel", h, tg)] if br == 1 else [])
            P.op("pe", lambda e: e.matmul(out=banks[sbk][:, c0:c1], lhsT=lhs, rhs=qA[:, h, tg * 512 + c0:tg * 512 + c1], start=True, stop=(len(masks) == 0)),
                 reads=kcell + qcells, writes=[bk(sbk)], inc=(len(masks) == 0))
            for mi, (m0, m1, mrhs) in enumerate(masks):
                last = (mi == len(masks) - 1)
                P.op("pe", lambda e: e.matmul(out=banks[sbk][:, m0:m1], lhsT=identb, rhs=mrhs, start=False, stop=last, skip_group_check=True),
                     reads=["cbs", "cmpneg"], writes=[bk(sbk)], inc=last)
            return sbk

        def emit_exp(it, sbk):
            blk = it["blk"]
            h, br, tg = blk["h"], blk["br"], blk["tg"]
            kt, c0, c1 = it["kt"], it["c0"], it["c1"]
            pi = pctr[0] % 4
            pctr[0] += 1
            if br == 0:
                bias = -slopes[h] * (512 * tg - 31)
            else:
                bias = -slopes[h] * (512 * tg - 128 * kt)
            P.op("act", lambda e: e.activation(out=pT[pi][:, c0:c1], in_=banks[sbk][:, c0:c1], func=AF.Exp, bias=float(bias), scale=0.125),
                 writes=[bk(sbk), ("pT", pi)])
            return pi

        def emit_pv(it, pi):
            blk = it["blk"]
            g, br, ab, nca = blk["g"], blk["br"], blk["ab"], blk["ncols"]
            kt, c0, c1 = it["kt"], it["c0"], it["c1"]
            ttls = list(range(c0 // 128, c1 // 128))
            for ttl in ttls:
                if br == 0:
                    rhs = vcA[:, g, :]
                    vcell = [("vcA", g), "vcA_c"]
                else:
                    rhs = vtok[:, kt, br - 1, g, :]
                    vcell = [("vtok", kt)]
                st = blk["first"]
                blk["first"] = False
                P.op("pe", lambda e: e.matmul(out=banks[ab][:, ttl * nca:(ttl + 1) * nca], lhsT=pT[pi][:, ttl * 128:(ttl + 1) * 128], rhs=rhs,
                                              start=st, stop=True, skip_group_check=True),
                     reads=[("pT", pi)] + vcell, writes=[bk(ab)], inc=(ttl == ttls[-1]))

        ntmp = view(WST_B, 1024, F32, "p (a d) -> p a d", a=4)

        def normalize(blk):
            h, g, br, tg, ab, ncols_acc = blk["h"], blk["g"], blk["br"], blk["tg"], blk["ab"], blk["ncols"]
            accv = banks[ab][:, 0:4 * ncols_acc].rearrange("p (a n) -> p a n", a=4)
            P.op("dve", lambda e: e.tensor_scalar(out=r4[:], in0=accv[:, :, 64], scalar1=1e-30, scalar2=None, op0=ALU.max), writes=[bk(ab), "r4"])
            P.op("dve", lambda e: e.reciprocal(out=r4[:], in_=r4[:]), writes=["r4"])
            P.op("dve", lambda e: e.tensor_tensor(out=gr4[:], in0=r4[:], in1=gates[:, tg * 4:tg * 4 + 4, h * 3 + br], op=ALU.mult),
                 reads=["r4"] + [("gates", tg * 4 + a) for a in range(4)], writes=["gr4"])
            grb = gr4[:, :].unsqueeze(2).to_broadcast([128, 4, 64])
            osl = o_tok[:, :, h * 64:(h + 1) * 64]
            ocells = [("o_tok", a) for a in range(4)]
            if br == 0:
                P.op("dve", lambda e: e.tensor_tensor(out=osl, in0=accv[:, :, 0:64], in1=grb, op=ALU.mult), reads=["gr4"], writes=[bk(ab)] + ocells)
                rb = r4[:, :].unsqueeze(2).to_broadcast([128, 4, 32])
                isl = impacc[:, :, g, :]
                icells = [("impacc", a, g) for a in range(4)]
                if h % 4 == 0:
                    P.op("dve", lambda e: e.tensor_tensor(out=isl, in0=accv[:, :, 65:97], in1=rb, op=ALU.mult), reads=["r4"], writes=[bk(ab)] + icells)
                else:
                    P.op("dve", lambda e: e.tensor_tensor(out=ntmp[:, :, 0:32], in0=accv[:, :, 65:97], in1=rb, op=ALU.mult), reads=["r4"], writes=[bk(ab), "ntmp"])
                    P.op("dve", lambda e: e.tensor_tensor(out=isl, in0=isl, in1=ntmp[:, :, 0:32], op=ALU.add), reads=["ntmp"], writes=icells)
            else:
                P.op("dve", lambda e: e.tensor_tensor(out=ntmp, in0=accv[:, :, 0:64], in1=grb, op=ALU.mult), reads=["gr4"], writes=[bk(ab), "ntmp"])
                P.op("dve", lambda e: e.tensor_tensor(out=osl, in0=osl, in1=ntmp, op=ALU.add), reads=["ntmp"], writes=ocells)

        def selection(tg):
            psT = banks[5][:].bitcast(BF16)
            for g in range(2):
                for a in range(4):
                    tt = tg * 4 + a
                    P.op("dve", lambda e: e.tensor_tensor(out=impf[:], in0=impacc[:, a, g, :], in1=bonus[:, tt, :], op=ALU.add),
                         reads=[("impacc", a, g), "bonus"], writes=["impf"])
                    P.op("dve", lambda e: e.max(out=m8[:], in_=impf[:]), reads=["impf"], writes=["m8"])
                    P.op("dve", lambda e: e.match_replace(out=scr32[:], in_to_replace=m8[:], in_values=impf[:], imm_value=-3.0e38), reads=["impf", "m8"], writes=["scr32"])
                    P.op("dve", lambda e: e.max(out=m8[:], in_=scr32[:]), reads=["scr32"], writes=["m8"])
                    P.op("dve", lambda e: e.tensor_scalar(out=Maug[:, a, :], in0=impf[:], scalar1=m8[:, 7:8], scalar2=-BIG, op0=ALU.is_lt, op1=ALU.mult),
                         reads=["impf", "m8"], writes=[("Maug", a)])
                    P.op("pe", lambda e: e.transpose(out=psT[0:32, a * 128:(a + 1) * 128], in_=Maug[:, a, :], identity=identb),
                         reads=[("Maug", a), "cbs"], writes=[bk(5)], inc=True)
                for j in range(4):
                    hh = g * 4 + j
                    if j % 2 == 0:
                        P.op("act", lambda e: e.activation(out=qA[0:32, hh, tg * 512:(tg + 1) * 512], in_=psT[0:32, 0:512], func=AF.Copy),
                             reads=["qA_aug"], writes=[bk(5), ("qsel", hh, tg)])
                    else:
                        P.op("dve", lambda e: e.tensor_copy(out=qA[0:32, hh, tg * 512:(tg + 1) * 512], in_=psT[0:32, 0:512]),
                             reads=["qA_aug"], writes=[bk(5), ("qsel", hh, tg)])

        for tg in range(4):
            items = []

            def add_block(h, br, kts, ncols):
                blk = dict(h=h, g=h // 4, br=br, tg=tg, ncols=ncols, ab=3 + bctr[0] % 3, first=True)
                bctr[0] += 1
                for i, kt in enumerate(kts):
                    if br == 0:
                        c0, c1 = 0, 512
                    elif br == 1:
                        c0, c1 = max(0, 128 * kt - 512 * tg), 512
                    else:
                        c0 = max(0, 128 * kt - 512 * tg)
                        c1 = min(512, 128 * (kt + 5) - 512 * tg)
                    items.append(dict(blk=blk, kt=kt, c0=c0, c1=c1, last=(i == len(kts) - 1)))

            for h in range(NH):
                add_block(h, 0, [0], 97)
            for h in range(NH):
                add_block(h, 2, list(range(max(0, 4 * tg - 4), 4 * tg + 4)), 65)
            for h in range(NH):
                add_block(h, 1, list(range(0, 4 * tg + 4)), 65)

            pend = []

            def drain_one():
                it, pi = pend.pop(0)
                emit_pv(it, pi)
                if it["last"]:
                    normalize(it["blk"])
                    if it["blk"]["br"] == 0 and it["blk"]["h"] == NH - 1:
                        selection(tg)

            for it in items:
                sbk = emit_qk(it)
                pi = emit_exp(it, sbk)
                pend.append((it, pi))
                if len(pend) > LOOK:
                    drain_one()
            while pend:
                drain_one()
            for a in range(4):
                ob = 6 + otr[0] % 2
                otr[0] += 1
                for c in range(4):
                    P.op("pe", lambda e, a=a, c=c, ob=ob: e.transpose(out=banks[ob][:, c * 128:(c + 1) * 128], in_=o_tok[:, a, c * 128:(c + 1) * 128], identity=identf[:]),
                         reads=[("o_tok", a), "identf"], writes=[bk(ob)], inc=(c == 3))
                t0 = tg * 512 + a * 128
                P.op("act", lambda e, ob=ob, t0=t0: e.activation(out=oT[:, :, t0:t0 + 128], in_=banks[ob][:].rearrange("p (c t) -> p c t", c=4), func=AF.Copy),
                     writes=[bk(ob), ("oT", tg)])
        P.barrier()
        if debug and l == 0:
            for c in range(4):
                for hf in range(2):
                    P.op("dve", lambda e, c=c, hf=hf: e.tensor_copy(out=ytok[0], in_=oT[:, c, hf * 1024:(hf + 1) * 1024]), writes=["ytok0"])
                    P.barrier()
                    P.dma("sp", "dbg_o", lambda e, c=c, hf=hf: e.dma_start(out=dbg["d_o"][:, c * S + hf * 1024:c * S + (hf + 1) * 1024], in_=ytok[0]), reads=["ytok0"])
                    P.barrier()
        if stop_after == "B":
            break

        rmsnorm(hT_C, pb0 + 0, "hT")
        P.op("pool", lambda e: e.memset(zT[:, :, 0:30], 0.0), writes=["zpad"])
        wsC = WStream(l, [(("UC", c), "p (k n) -> p k n", dict(k=8), "pool") for c in range(4)])
        for c in range(4):
            w, wc = wsC.get(c)
            for tg in range(4):
                ts = slice(tg * 512, (tg + 1) * 512)
                for half in range(2):
                    pb = half
                    for k in range(8):
                        P.op("pe", lambda e, k=k, half=half, pb=pb, w=w: e.matmul(out=banks[pb][:], lhsT=w[:, k, half * 128:(half + 1) * 128], rhs=hT_C[:, k, ts],
                                                                                 start=(k == 0), stop=(k == 7)),
                             reads=[wc, ("hT", k, tg)], writes=[bk(pb)], inc=(k == 7))
                P.op("act", lambda e: e.activation(out=fsil, in_=banks[1][:], func=AF.Sigmoid), writes=[bk(1), "fsil"])
                P.op("dve", lambda e, c=c, tg=tg: e.tensor_tensor(out=zT[:, c, 30 + tg * 512:30 + (tg + 1) * 512], in0=banks[0][:], in1=fsil, op=ALU.mult),
                     reads=["fsil"], writes=[bk(0), ("zT", c, tg)])
        cw0 = pb0 + 16
        dctr = [0]
        dgb = [smv(0, 7936, BF16, "p (k n) -> p k n", k=31), view(WST_B, 7936, BF16, "p (k n) -> p k n", k=31)]
        P.barrier()
        zc2 = [zc_R3, view(WST_B + 8192, 8192, F32, "p (c t) -> p c t", c=4)]
        def conv_tg(tg):
            zc = zc2[tg % 2]
            zx = [("wst", 1)] if tg % 2 == 1 else []
            for c in range(4):
                zr = [("zT", c, t_) for t_ in range(max(0, tg - 1), tg + 1)] + ["zpad"]
                di = dctr[0] % 2
                dctr[0] += 1
                dg = dgb[di]
                dx = [("wst", 0)] if di == 1 else []
                for k in range(31):
                    P.op("dve", lambda e: e.tensor_scalar(out=dg[:, k, :], in0=identb, scalar1=prm[:, cw0 + c * 31 + k:cw0 + c * 31 + k + 1], scalar2=None, op0=ALU.mult),
                         reads=["cbs", "prm"], writes=[("dg", di, k)])
                cb_ = 4 + dctr[0] % 4
                for k in range(31):
                    P.op("pe", lambda e: e.matmul(out=banks[cb_][:], lhsT=dg[:, k, :], rhs=zT[:, c, tg * 512 + k:tg * 512 + k + 512], start=(k == 0), stop=(k == 30)),
                         reads=zr + [("dg", di, k)], writes=[bk(cb_)], inc=(k == 30))
                P.op("act", lambda e: e.activation(out=zc[:, c, :], in_=banks[cb_][:], func=AF.Identity, bias=prm[:, pb0 + 140 + c:pb0 + 141 + c], scale=1.0),
                     reads=["prm"], writes=[bk(cb_), ("zc", tg % 2, c)])

        def ln_tg(tg):
            zc = zc2[tg % 2]
            if debug and l == 0:
                for c in range(4):
                    P.barrier()
                    P.dma("sp", "dbg_zc", lambda e, c=c, tg=tg: e.dma_start(out=dbg["d_zc"][:, c * S + tg * 512:c * S + (tg + 1) * 512], in_=zc[:, c, :]), reads=[("zc", tg % 2, c)])
                    P.barrier()
            for c in range(4):
                q = c % 2
                P.op("act", lambda e, c=c, q=q: e.activation(out=sqb[q][:], in_=zc[:, c, :], func=AF.Copy), reads=[("zc", tg % 2, c)], writes=[("sqb", q)])
                P.op("pe", lambda e, c=c, q=q: e.matmul(out=banks[2][:], lhsT=onesb, rhs=sqb[q][:], start=(c == 0), stop=(c == 3)),
                     reads=[("sqb", q), "cbs"], writes=[bk(2)], inc=True)
            for c in range(4):
                q = c % 2
                P.op("act", lambda e, c=c, q=q: e.activation(out=sqb[q][:], in_=zc[:, c, :], func=AF.Square), reads=[("zc", tg % 2, c)], writes=[("sqb", q)])
                P.op("pe", lambda e, c=c, q=q: e.matmul(out=banks[3][:], lhsT=onesb, rhs=sqb[q][:], start=(c == 0), stop=(c == 3)),
                     reads=[("sqb", q), "cbs"], writes=[bk(3)], inc=True)
            mean = lnt[:, 0, :]
            var = lnt[:, 1, :]
            tmp = lnt[:, 2, :]
            P.op("act", lambda e: e.activation(out=mean, in_=banks[2][:], func=AF.Copy, scale=1.0 / 512), writes=[bk(2), "mean"])
            P.op("dve", lambda e: e.tensor_tensor(out=tmp, in0=mean, in1=mean, op=ALU.mult), reads=["mean"], writes=["lntmp"])
            P.op("dve", lambda e: e.scalar_tensor_tensor(out=var, in0=banks[3][:], scalar=1.0 / 512, in1=tmp, op0=ALU.mult, op1=ALU.subtract),
                 reads=["lntmp"], writes=[bk(3), "var"])
            P.op("dve", lambda e: e.tensor_scalar(out=var, in0=var, scalar1=0.0, scalar2=None, op0=ALU.max), writes=["var"])
            P.op("act", lambda e: e.activation(out=var, in_=var, func=AF.Sqrt, bias=EPS, scale=1.0), writes=["var"])
            P.op("dve", lambda e: e.reciprocal(out=var, in_=var), writes=["var"])
            for c in range(4):
                P.op("dve", lambda e, c=c: e.tensor_tensor(out=zc[:, c, :], in0=zc[:, c, :], in1=mean, op=ALU.subtract), reads=["mean"], writes=[("zc", tg % 2, c)])
                P.op("dve", lambda e, c=c: e.tensor_tensor(out=zc[:, c, :], in0=zc[:, c, :], in1=var, op=ALU.mult), reads=["var"], writes=[("zc", tg % 2, c)])
                P.op("act", lambda e, c=c, tg=tg: e.activation(out=sT[:, c, tg * 512:(tg + 1) * 512], in_=zc[:, c, :], func=AF.Silu,
                                                             scale=prm[:, pb0 + 144 + c:pb0 + 145 + c], bias=prm[:, pb0 + 148 + c:pb0 + 149 + c]),
                     reads=[("zc", tg % 2, c), "prm"], writes=[("sT", c, tg)])

        conv_tg(0)
        for tg in range(4):
            if tg + 1 < 4:
                conv_tg(tg + 1)
            ln_tg(tg)
        P.barrier()
        if debug and l == 0:
            for c in range(4):
                for hf in range(2):
                    P.op("dve", lambda e, c=c, hf=hf: e.tensor_copy(out=ytok[0], in_=sT[:, c, hf * 1024:(hf + 1) * 1024]), writes=["ytok0"])
                    P.barrier()
                    P.dma("sp", "dbg_s", lambda e, c=c, hf=hf: e.dma_start(out=dbg["d_s"][:, c * S + hf * 1024:c * S + (hf + 1) * 1024], in_=ytok[0]), reads=["ytok0"])
                    P.barrier()
        if stop_after == "C":
            break

        facc2 = [[facc[0], facc[1]], [smv(0, 2048, F32), smv(2048, 2048, F32)]]
        dit = 0
        wsm = smv(8208, 2048, BF16)
        wdma(l, ("D1a", 0), 0)
        wcast(("D1a", 0), 0, wbf[0], ("wbf", 0), "act")
        wdma(l, ("D1b", 0), 1)
        wcast(("D1b", 0), 1, wsm, "wsm", "pool")
        for oc in range(8):
            wa = wbf[oc % 2][:, 0:2048].rearrange("p (k n) -> p k n", k=8)
            wac = ("wbf", oc % 2)
            wb_ = wsm[:, 0:1024].rearrange("p (k b n) -> p k b n", k=4, b=2)
            wbc = "wsm"
            if oc + 1 < 8:
                wdma(l, ("D1a", oc + 1), 0)
                wcast(("D1a", oc + 1), 0, wbf[(oc + 1) % 2], ("wbf", (oc + 1) % 2), "pool")
                wdma(l, ("D1b", oc + 1), 1)
            for tg in range(4):
                ts = slice(tg * 512, (tg + 1) * 512)
                bb = 4 * (dit % 2)
                fa = facc2[dit % 2]
                fc = ("faccD", dit % 2)
                dit += 1
                for half in range(2):
                    for k in range(8):
                        P.op("pe", lambda e: e.matmul(out=banks[bb + half][:], lhsT=wa[:, k, half * 128:(half + 1) * 128], rhs=hT_C[:, k, ts], start=(k == 0), stop=(k == 7)),
                             reads=[wac, ("hT", k, tg)], writes=[bk(bb + half)], inc=(k == 7))
                for k in range(4):
                    P.op("pe", lambda e: e.matmul(out=banks[bb + 2][:], lhsT=wb_[:, k, 0, :], rhs=oT[:, k, ts], start=(k == 0), stop=(k == 3)),
                         reads=[wbc, ("oT", tg)], writes=[bk(bb + 2)], inc=(k == 3))
                for k in range(4):
                    P.op("pe", lambda e: e.matmul(out=banks[bb + 3][:], lhsT=wb_[:, k, 1, :], rhs=sT[:, k, ts], start=(k == 0), stop=(k == 3)),
                         reads=[wbc, ("sT", k, tg)], writes=[bk(bb + 3)], inc=(k == 3))
                P.op("act", lambda e: e.activation(out=fa[0], in_=banks[bb][:], func=AF.Sigmoid), writes=[bk(bb), fc + (0,)])
                P.op("act", lambda e: e.activation(out=fa[1], in_=banks[bb + 1][:], func=AF.Sigmoid), writes=[bk(bb + 1), fc + (1,)])
                P.op("dve", lambda e: e.tensor_tensor(out=fa[0], in0=banks[bb + 2][:], in1=fa[0], op=ALU.mult), writes=[bk(bb + 2), fc + (0,)])
                P.op("dve", lambda e: e.tensor_tensor(out=fa[1], in0=banks[bb + 3][:], in1=fa[1], op=ALU.mult), writes=[bk(bb + 3), fc + (1,)])
                P.op("pool", lambda e: e.tensor_tensor(out=mT[:, oc, tg * 512:(tg + 1) * 512], in0=fa[0], in1=fa[1], op=ALU.add),
                     reads=[fc + (0,), fc + (1,)], writes=[("mT", oc, tg)])
            if oc + 1 < 8:
                wcast(("D1b", oc + 1), 1, wsm, "wsm", "act")
        wsO = WStream(l, [(("WO", oc), "p (k n) -> p k n", dict(k=8), "pool") for oc in range(8)])
        for oc in range(8):
            w, wc = wsO.get(oc)
            for tg in range(4):
                ts = slice(tg * 512, (tg + 1) * 512)
                pb = 4 + (oc * 4 + tg) % 2
                for k in range(8):
                    P.op("pe", lambda e, k=k, pb=pb, w=w: e.matmul(out=banks[pb][:], lhsT=w[:, k, :], rhs=mT[:, k, ts], start=(k == 0), stop=(k == 7)),
                         reads=[wc, ("mT", k, tg)], writes=[bk(pb)], inc=(k == 7))
                P.op("dve", lambda e, oc=oc, pb=pb: e.tensor_tensor(out=xT[:, oc, ts], in0=banks[pb][:], in1=xT[:, oc, ts], op=ALU.add),
                     writes=[bk(pb), ("xT", oc, tg)])
        P.barrier()
        if debug and l == 0:
            for c in range(8):
                P.dma("sp", "dbg_xm", lambda e, c=c: e.dma_start(out=dbg["d_xmix"][:, c * S:(c + 1) * S], in_=xT[:, c, :]), reads=[("xT", c, t_) for t_ in range(4)])
            P.barrier()
        if stop_after == "D":
            break

        rmsnorm(hT_C, pb0 + 8, "hT")
        fw0 = pb0 + 152
        fb0 = pb0 + 284
        itc = 0
        especs = []
        for th in range(2):
            especs += [(("UP", c), "p (k n) -> p k n", dict(k=8), "act") for c in range(22)]
            for oc in range(8):
                especs += [(("DN", oc, 0), "p (k n) -> p k n", dict(k=11), "pool"), (("DN", oc, 1), "p (k n) -> p k n", dict(k=11), "act")]
        wsE = WStream(l, especs, nslots=3, dist=2)
        epend = [None]
        for th in range(2):
            for c in range(22):
                w, wc = wsE.get(th * 38 + c)
                dg = edg[c % 2]
                cha, chv = c, c + 22
                wa2 = prm[:, fw0 + cha * 3 + 2:fw0 + cha * 3 + 3]
                wa1 = prm[:, fw0 + cha * 3 + 1:fw0 + cha * 3 + 2]
                wa0 = prm[:, fw0 + cha * 3:fw0 + cha * 3 + 1]
                for k in range(3):
                    P.op("dve", lambda e: e.tensor_scalar(out=dg[:, k, :], in0=identb, scalar1=prm[:, fw0 + chv * 3 + k:fw0 + chv * 3 + k + 1], scalar2=None, op0=ALU.mult),
                         reads=["cbs", "prm"], writes=[("edg", c % 2, k)])
                for tq in range(2):
                    tg = th * 2 + tq
                    ts = slice(tg * 512, (tg + 1) * 512)
                    par = itc % 2
                    itc += 1
                    ua, uv, cv = par, 2 + par, 4 + par
                    ac = eacc[par]
                    ub = eub[par]
                    for k in range(8):
                        P.op("pe", lambda e: e.matmul(out=banks[ua][:], lhsT=w[:, k, 0:128], rhs=hT_C[:, k, ts], start=(k == 0), stop=(k == 7)),
                             reads=[wc, ("hT", k, tg)], writes=[bk(ua)], inc=(k == 7))
                    if tq == 0 and tg > 0:
                        for k in range(8):
                            P.op("pe", lambda e: e.matmul(out=banks[6][:, 0:2], lhsT=w[:, k, 0:128], rhs=hT_C[:, k, tg * 512 - 2:tg * 512], start=(k == 0), stop=(k == 7)),
                                 reads=[wc, ("hT", k, tg - 1)], writes=[bk(6)], inc=(k == 7))
                        P.op("dve", lambda e: e.tensor_copy(out=ecrA[:], in_=banks[6][:, 0:2]), writes=[bk(6), "ecrA"])
                    for k in range(8):
                        P.op("pe", lambda e: e.matmul(out=banks[uv][:], lhsT=w[:, k, 128:256], rhs=hT_C[:, k, ts], start=(k == 0), stop=(k == 7)),
                             reads=[wc, ("hT", k, tg)], writes=[bk(uv)], inc=(k == 7))
                    if tg == 0:
                        P.op("dve", lambda e: e.memset(ub[:, 0:2], 0.0), writes=[("eub", par)])
                    elif tq == 0:
                        for k in range(8):
                            P.op("pe", lambda e: e.matmul(out=banks[7][:, 0:2], lhsT=w[:, k, 128:256], rhs=hT_C[:, k, tg * 512 - 2:tg * 512], start=(k == 0), stop=(k == 7)),
                                 reads=[wc, ("hT", k, tg - 1)], writes=[bk(7)], inc=(k == 7))
                        P.op("dve", lambda e: e.tensor_copy(out=ub[:, 0:2], in_=banks[7][:, 0:2]), writes=[bk(7), ("eub", par)])
                    else:
                        P.op("dve", lambda e: e.tensor_copy(out=ub[:, 0:2], in_=eub[1 - par][:, 512:514]), reads=[("eub", 1 - par)], writes=[("eub", par)])
                    P.op("act", lambda e: e.activation(out=ac, in_=banks[ua][:], func=AF.Identity, scale=wa2, bias=prm[:, fb0 + cha:fb0 + cha + 1]),
                         reads=["prm"], writes=[bk(ua), ("eacc", par)])
                    P.op("act", lambda e: e.activation(out=ub[:, 2:514], in_=banks[uv][:], func=AF.Copy), writes=[bk(uv), ("eub", par)])
                    P.op("dve", lambda e: e.scalar_tensor_tensor(out=ac[:, 1:512], in0=banks[ua][:, 0:511], scalar=wa1, in1=ac[:, 1:512], op0=ALU.mult, op1=ALU.add),
                         writes=[bk(ua), ("eacc", par)])
                    crcur = ecrA if tq == 0 else ecrB
                    crn = "ecrA" if tq == 0 else "ecrB"
                    if tq == 0:
                        P.op("dve", lambda e: e.tensor_copy(out=ecrB[:], in_=banks[ua][:, 510:512]), writes=[bk(ua), "ecrB"])
                    P.op("dve", lambda e: e.scalar_tensor_tensor(out=ac[:, 2:512], in0=banks[ua][:, 0:510], scalar=wa0, in1=ac[:, 2:512], op0=ALU.mult, op1=ALU.add),
                         writes=[bk(ua), ("eacc", par)])
                    if tg > 0:
                        P.op("dve", lambda e: e.scalar_tensor_tensor(out=ac[:, 0:1], in0=crcur[:, 1:2], scalar=wa1, in1=ac[:, 0:1], op0=ALU.mult, op1=ALU.add),
                             reads=[crn], writes=[("eacc", par)])
                        P.op("dve", lambda e: e.scalar_tensor_tensor(out=ac[:, 0:2], in0=crcur[:, 0:2], scalar=wa0, in1=ac[:, 0:2], op0=ALU.mult, op1=ALU.add),
                             reads=[crn], writes=[("eacc", par)])
                    P.op("act", lambda e: e.activation(out=esil[par], in_=ac, func=AF.Silu), reads=[("eacc", par)], writes=[("esil", par)])

                    def mk_part2(c=c, tq=tq, par=par, ub=ub, dg=dg, cv=cv, chv=chv):
                        def part2():
                            for k in range(3):
                                P.op("pe", lambda e: e.matmul(out=banks[cv][:], lhsT=dg[:, k, :], rhs=ub[:, k:k + 512], start=(k == 0), stop=(k == 2)),
                                     reads=[("edg", c % 2, k), ("eub", par)], writes=[bk(cv)], inc=(k == 2))
                            P.op("dve", lambda e: e.scalar_tensor_tensor(out=gT[:, c, tq * 512:(tq + 1) * 512], in0=banks[cv][:], scalar=prm[:, fb0 + chv:fb0 + chv + 1],
                                                                         in1=esil[par], op0=ALU.add, op1=ALU.mult),
                                 reads=[("esil", par), "prm"], writes=[bk(cv), ("gT", c, tq)])
                        return part2

                    if epend[0] is not None:
                        epend[0]()
                    epend[0] = mk_part2()
            if epend[0] is not None:
                epend[0]()
                epend[0] = None
            for oc in range(8):
                for hf in range(2):
                    wk, wkc = wsE.get(th * 38 + 22 + oc * 2 + hf)
                    for tq in range(2):
                        pb = 4 + (oc % 2) * 2 + tq
                        for kk in range(11):
                            k = hf * 11 + kk
                            P.op("pe", lambda e: e.matmul(out=banks[pb][:], lhsT=wk[:, kk, :], rhs=gT[:, k, tq * 512:(tq + 1) * 512],
                                                          start=(k == 0), stop=(k == 21)),
                                 reads=[wkc, ("gT", k, tq)], writes=[bk(pb)], inc=(kk == 10))
                for tq in range(2):
                    tg = th * 2 + tq
                    pb = 4 + (oc % 2) * 2 + tq
                    P.op("dve", lambda e: e.tensor_tensor(out=xT[:, oc, tg * 512:(tg + 1) * 512], in0=banks[pb][:], in1=xT[:, oc, tg * 512:(tg + 1) * 512], op=ALU.add),
                         writes=[bk(pb), ("xT", oc, tg)])
        P.barrier()
        if debug and l == 0:
            for c in range(8):
                P.dma("sp", "dbg_xf", lambda e, c=c: e.dma_start(out=dbg["d_xffn"][:, c * S:(c + 1) * S], in_=xT[:, c, :]), reads=[("xT", c, t_) for t_ in range(4)])
            P.barrier()

    yT = view(R2_B, 32768, F32, "p (c t) -> p c t", c=8)
    for th in range(2):
        for tq in range(2):
            tg = th * 2 + tq
            ts = slice(tg * 512, (tg + 1) * 512)
            for c in range(8):
                q = c % 2
                P.op("act", lambda e, c=c, q=q: e.activation(out=sqb[q][:], in_=xT[:, c, ts], func=AF.Square), reads=[("xT", c, tg)], writes=[("sqb", q)])
                P.op("pe", lambda e, c=c, q=q: e.matmul(out=banks[6][:], lhsT=onesb, rhs=sqb[q][:], start=(c == 0), stop=(c == 7)),
                     reads=[("sqb", q), "cbs"], writes=[bk(6)], inc=True)
            P.op("act", lambda e: e.activation(out=rstd[:], in_=banks[6][:], func=AF.Sqrt, bias=EPS, scale=1.0 / D), writes=[bk(6), "rstd"])
            P.op("dve", lambda e: e.reciprocal(out=rstd[:], in_=rstd[:]), writes=["rstd"])
            for c in range(8):
                P.op("dve", lambda e, c=c, tq=tq: e.scalar_tensor_tensor(out=yT[:, c, tq * 512:(tq + 1) * 512], in0=xT[:, c, ts], scalar=prm[:, 2 * NPRM_L + c:2 * NPRM_L + c + 1],
                                                                        in1=rstd[:], op0=ALU.mult, op1=ALU.mult),
                     reads=[("xT", c, tg), "rstd", "prm"], writes=[("yT", c, tq)])
        for tl in range(8):
            tt = th * 8 + tl
            yb = tt % 2
            for half in range(2):
                pb = (tt * 2 + half) % 4
                for j in range(4):
                    c = half * 4 + j
                    P.op("pe", lambda e, c=c, j=j, pb=pb, tl=tl: e.transpose(out=banks[pb][:, j * 128:(j + 1) * 128], in_=yT[:, c, tl * 128:(tl + 1) * 128], identity=identf[:]),
                         reads=[("yT", c, tl // 4), "identf"], writes=[bk(pb)], inc=(j == 3))
                if half == 0:
                    P.op("act", lambda e, yb=yb, pb=pb: e.activation(out=ytok[yb][:, 0:512], in_=banks[pb][:], func=AF.Copy), writes=[bk(pb), ("ytok", yb, 0)])
                else:
                    P.op("dve", lambda e, yb=yb, pb=pb: e.tensor_copy(out=ytok[yb][:, 512:1024], in_=banks[pb][:]), writes=[bk(pb), ("ytok", yb, 1)])
            P.dma("sp", "out%d" % yb, lambda e, tt=tt, yb=yb: e.dma_start(out=out_d[tt * 128:(tt + 1) * 128, :], in_=ytok[yb]),
                  reads=[("ytok", yb, 0), ("ytok", yb, 1)])
        P.barrier()
    P.barrier()
    P.emit()
    es.close()
    return nc


def _pack_weights(inp):
    wpk = np.zeros((L, 128, TOT), np.float32)
    for l in range(L):
        w_in = inp["w_in"][l]
        def kc(wcols):
            n = wcols.shape[1]
            return wcols.reshape(8, 128, n).transpose(1, 0, 2).reshape(128, 8 * n)
        def put(name, arr):
            off, e = UNITS[name]
            assert arr.shape == (128, e), (name, arr.shape, e)
            wpk[l, :, off:off + e] = arr
        fm = np.concatenate([w_in[:, 0:512], w_in[:, 512:640], w_in[:, 640:768], w_in[:, 768:896], w_in[:, 1024:1152]], axis=1)
        for u in range(4):
            put(("FM", u), kc(fm[:, u * 256:(u + 1) * 256]))
        put("TOKA", kc(np.concatenate([w_in[:, 896:1024], w_in[:, 1152:1280]], axis=1)))
        put("TOKB", kc(w_in[:, 1280:1304]))
        uc = w_in[:, 1304:2328]
        for c in range(4):
            put(("UC", c), kc(np.concatenate([uc[:, c * 128:(c + 1) * 128], uc[:, 512 + c * 128:512 + (c + 1) * 128]], axis=1)))
        gm = w_in[:, 2328:4376]
        wab = inp["w_attn_br"][l]
        wcb = inp["w_conv_br"][l]
        for oc in range(8):
            put(("D1a", oc), kc(np.concatenate([gm[:, oc * 128:(oc + 1) * 128], gm[:, 1024 + oc * 128:1024 + (oc + 1) * 128]], axis=1)))
            a = wab[:, oc * 128:(oc + 1) * 128].reshape(4, 128, 128).transpose(1, 0, 2)
            b = wcb[:, oc * 128:(oc + 1) * 128].reshape(4, 128, 128).transpose(1, 0, 2)
            put(("D1b", oc), np.stack([a, b], axis=2).reshape(128, 1024))
            put(("WO", oc), kc(inp["w_out"][l][:, oc * 128:(oc + 1) * 128]))
        wup = inp["ffn_w_up"][l]
        for c in range(22):
            put(("UP", c), kc(np.concatenate([wup[:, c * 128:(c + 1) * 128], wup[:, FFN + c * 128:FFN + (c + 1) * 128]], axis=1)))
        wdn = inp["ffn_w_down"][l]
        for oc in range(8):
            blk = wdn[:, oc * 128:(oc + 1) * 128].reshape(22, 128, 128).transpose(1, 0, 2)
            put(("DN", oc, 0), blk[:, 0:11].reshape(128, 1408))
            put(("DN", oc, 1), blk[:, 11:22].reshape(128, 1408))
        for kv, nm in enumerate(("cmp_k_w1", "cmp_v_w1")):
            w1 = inp[nm][l].reshape(32, 64, 128).transpose(1, 0, 2)
            w1 = np.concatenate([w1, w1], axis=0)
            put(("CW", kv, 0), w1[:, 0:16].reshape(128, 2048))
            put(("CW", kv, 1), w1[:, 16:32].reshape(128, 2048))
        put("W2", np.concatenate([inp["cmp_k_w2"][l], inp["cmp_v_w2"][l]], axis=1))
    return wpk


def _pack_params(inp):
    prm = np.zeros((128, NPRM), np.float32)
    for l in range(L):
        b = l * NPRM_L
        prm[:, b:b + 8] = inp["norm1_g"][l].reshape(8, 128).T
        prm[:, b + 8:b + 16] = inp["norm2_g"][l].reshape(8, 128).T
        prm[:, b + 16:b + 140] = inp["conv_dw_w"][l].reshape(31, 4, 128).transpose(2, 1, 0).reshape(128, 124)
        prm[:, b + 140:b + 144] = inp["conv_dw_b"][l].reshape(4, 128).T
        prm[:, b + 144:b + 148] = inp["conv_ln_g"][l].reshape(4, 128).T
        prm[:, b + 148:b + 152] = inp["conv_ln_b"][l].reshape(4, 128).T
        prm[:, b + 152:b + 284] = inp["ffn_dw_w"][l].reshape(3, 44, 128).transpose(2, 1, 0).reshape(128, 132)
        prm[:, b + 284:b + 328] = inp["ffn_dw_b"][l].reshape(44, 128).T
        pk = inp["cmp_pe_k"][l].T
        pv = inp["cmp_pe_v"][l].T
        prm[:, b + 328:b + 360] = np.concatenate([pk, pk], axis=0)
        prm[:, b + 360:b + 392] = np.concatenate([pv, pv], axis=0)
    prm[:, 2 * NPRM_L:2 * NPRM_L + 8] = inp["final_g"].reshape(8, 128).T
    return prm


def _constants():
    cf = np.zeros((128, 640), np.float32)
    cf[:, 0:128] = np.eye(128, dtype=np.float32)
    bonus = np.zeros((128, 16, 32), np.float32)
    for tt in range(16):
        t = tt * 128 + np.arange(128)
        cur = t // 64
        blk = np.arange(32)[None, :]
        forced = (blk == 0) | (blk == cur[:, None]) | (blk == cur[:, None] - 1)
        bonus[:, tt, :] = np.where(blk <= cur[:, None], 1.0e4 * forced, -1.0e30)
    cf[:, 128:640] = bonus.reshape(128, 512)

    cb = np.zeros((128, NCB), np.float32)
    cb[:, CB_IDENT:CB_IDENT + 128] = np.eye(128)
    cb[:, CB_ONES:CB_ONES + 128] = 1.0
    n = np.arange(128)[:, None]
    t = np.arange(128)[None, :]
    cb[:, CB_CAUS:CB_CAUS + 128] = np.where(t >= n, 0.0, -BIG)
    cb[:, CB_EDGE:CB_EDGE + 128] = np.where(t < n, 0.0, -BIG)
    tt_ = np.arange(S)[None, :]
    cb[:, CB_CMPNEG:CB_CMPNEG + S] = np.where((tt_ >= 16 * n + 31) & (n < 127), 0.0, -BIG)
    nn = np.arange(S)
    ks = np.zeros((64, S), np.float32)
    ks[nn // 64, nn] = 1.0
    ks[32, :] = nn % 128
    ks[33, :] = 1.0
    ks[34, :] = 1.0
    cb[0:64, CB_KSEL:CB_KSEL + S] = ks
    kw = ks.copy()
    kw[0:32, :] = 0.0
    cb[0:64, CB_KWIN:CB_KWIN + S] = kw
    kcm = np.zeros((64, 128), np.float32)
    kcm[32, :] = 16.0 * np.arange(128)
    kcm[33, :] = 1.0
    kcm[34, :] = 1.0
    cb[0:64, CB_KCMP:CB_KCMP + 128] = kcm
    ncmp = 127
    cs = np.arange(ncmp) * 16
    ce = cs + 31
    ss = np.arange(32) * 64
    se = ss + 63
    ov = np.minimum(ce[:, None], se[None, :]) - np.maximum(cs[:, None], ss[None, :]) + 1
    mp = np.clip(ov, 0, None).astype(np.float32) / 32.0
    cb[:, CB_MAP] = 1.0
    cb[0:127, CB_MAP + 1:CB_MAP + 33] = mp
    qa = np.zeros((64, 8, S), np.float32)
    tm = np.arange(S) % 512
    for h in range(8):
        sl = 2.0 ** (-(h + 1))
        qa[32, h, :] = 8.0 * sl
        qa[33, h, :] = -8.0 * sl * 128.0 * (tm // 128)
        qa[34, h, :] = -8.0 * sl * (tm % 128)
    cb[0:64, CB_QAUG:CB_QAUG + 8 * S] = qa.reshape(64, 8 * S)
    return cf, cb.astype(ml_dtypes.bfloat16)


_CACHE = {}


def kernel(**inputs):
    inp = {k: np.asarray(v) for k, v in inputs.items()}
    x = np.ascontiguousarray(inp["x"], dtype=np.float32)
    wpk = _pack_weights(inp)
    prm = _pack_params(inp)
    cf, cb = _constants()
    if "nc" not in _CACHE:
        _CACHE["nc"] = build_program()
    nc = _CACHE["nc"]
    in_maps = [{"x": x[b], "wpk": wpk, "prm": prm, "cf": cf, "cb": cb} for b in range(8)]
    res = run_bass_kernel_spmd(nc, in_maps, core_ids=list(range(8)))
    return np.stack([np.asarray(r["out"], dtype=np.float32) for r in res.results], axis=0)
```

```python
import numpy as np
import ml_dtypes
from contextlib import ExitStack
import concourse.bass as bass
import concourse.mybir as mybir
from concourse.bass_utils import run_bass_kernel_spmd

F32 = mybir.dt.float32
BF16 = mybir.dt.bfloat16
AF = mybir.ActivationFunctionType
ALU = mybir.AluOpType

S = 2048
D = 1024
L = 2
NH = 8
FFN = 2816
EPS = 1e-6
BIG = 32768.0
ENGS = ("pe", "act", "dve", "pool", "sp")
ATTACH_WAITS = True

UNITS = {}
_off = 0


def _u(name, e):
    global _off
    UNITS[name] = (_off, e)
    _off += e


for _i in range(4):
    _u(("FM", _i), 2048)
_u("TOKA", 2048)
_u("TOKB", 192)
for _i in range(4):
    _u(("UC", _i), 2048)
for _i in range(8):
    _u(("D1a", _i), 2048)
    _u(("D1b", _i), 1024)
for _i in range(8):
    _u(("WO", _i), 1024)
for _i in range(22):
    _u(("UP", _i), 2048)
for _i in range(8):
    for _h in range(2):
        _u(("DN", _i, _h), 1408)
for _kv in range(2):
    for _h in range(2):
        _u(("CW", _kv, _h), 2048)
_u("W2", 128)
TOT = _off

NPRM_L = 392
NPRM = 2 * NPRM_L + 8

CB_IDENT = 0
CB_ONES = 128
CB_CAUS = 256
CB_EDGE = 384
CB_CMPNEG = 512
CB_KSEL = 2560
CB_KWIN = 4608
CB_KCMP = 6656
CB_MAP = 6784
CB_QAUG = 6848
NCB = CB_QAUG + 8 * 2048


class _Rec:
    def __init__(self):
        self.call = None

    def __getattr__(self, name):
        def f(*a, **k):
            self.call = (name, a, k)
            return None
        return f


def _bind(fn):
    rec = _Rec()
    fn(rec)
    name, a, k = rec.call
    return lambda eng: getattr(eng, name)(*a, **k)


class Prog:
    def __init__(self, nc):
        self.nc = nc
        self.ops = {e: [] for e in ENGS}
        self.cnt = {e: 0 for e in ENGS}
        self.seen = {e: {} for e in ENGS}
        self.cells = {}
        self.streams = {}

    def _deps(self, reads, writes):
        need = {}

        def add(tok):
            if tok is None:
                return
            k, v = tok
            if need.get(k, 0) < v:
                need[k] = v

        for c in reads:
            st = self.cells.get(c)
            if st is not None:
                add(st[0])
        for c in writes:
            st = self.cells.get(c)
            if st is not None:
                add(st[0])
                for t in st[1]:
                    add(t)
        return need

    def _commit(self, tok, reads, writes):
        for c in reads:
            st = self.cells.get(c)
            if st is None:
                st = [None, []]
                self.cells[c] = st
            st[1].append(tok)
            if len(st[1]) > 16:
                m = {}
                for k, v in st[1]:
                    if m.get(k, 0) < v:
                        m[k] = v
                st[1] = list(m.items())
        for c in writes:
            self.cells[c] = [tok, []]

    def _waits(self, eng, need):
        waits = []
        seen = self.seen[eng]
        for k, v in need.items():
            if k == eng and eng == "pe":
                continue
            if seen.get(k, 0) >= v:
                continue
            seen[k] = v
            waits.append((k, v))
        return waits

    def op(self, eng, fn, reads=(), writes=(), inc=True):
        need = self._deps(reads, writes)
        waits = self._waits(eng, need)
        tok = (eng, self.cnt[eng] + 1)
        if inc:
            self.cnt[eng] += 1
        self.ops[eng].append((waits, _bind(fn), "eng" if inc else None, None))
        self._commit(tok, reads, writes)

    def dma(self, qeng, stream, fn, reads=(), writes=()):
        self.streams[stream] = self.streams.get(stream, 0) + 1
        need = self._deps(reads, writes)
        waits = self._waits(qeng, need)
        tok = (("dma", stream), 16 * self.streams[stream])
        self.ops[qeng].append((waits, _bind(fn), "dma", stream))
        self._commit(tok, reads, writes)

    def barrier(self):
        need = {e: self.cnt[e] for e in ENGS if self.cnt[e] > 0}
        for s, c in self.streams.items():
            need[("dma", s)] = 16 * c
        for e in ENGS:
            w = self._waits(e, dict(need))
            if w:
                self.ops[e].append((w, None, None, None))

    def emit(self):
        nc = self.nc
        with ExitStack() as es:
            sems = {}
            for e in ENGS:
                sems[e] = es.enter_context(nc.semaphore("s_" + e))
            for i, s in enumerate(self.streams):
                sems[("dma", s)] = es.enter_context(nc.semaphore("d%d" % i))
            block = es.enter_context(nc.Block())

            def run(eng_name):
                def body(eng):
                    for waits, fn, kind, stream in self.ops[eng_name]:
                        if fn is None:
                            for k, v in waits:
                                eng.wait_ge(sems[k], v)
                            continue
                        attach = None
                        if waits and ATTACH_WAITS and kind != "dma":
                            attach = waits[-1]
                            waits = waits[:-1]
                        for k, v in waits:
                            eng.wait_ge(sems[k], v)
                        ins = fn(eng)
                        if attach is not None:
                            ins._wait_ge(sems[attach[0]], attach[1])
                        if kind == "eng":
                            ins.then_inc(sems[eng_name], 1)
                        elif kind == "dma":
                            ins.then_inc(sems[("dma", stream)], 16)
                return body

            block.tensor(run("pe"))
            block.scalar(run("act"))
            block.vector(run("dve"))
            block.gpsimd(run("pool"))
            block.sync(run("sp"))


def build_program(debug=False, nlayers=L, stop_after=None):
    nc = bass.Bass("TRN2", target_bir_lowering=False)
    x_d = nc.dram_tensor("x", [S, D], F32, kind="ExternalInput").ap()
    wpk_d = nc.dram_tensor("wpk", [L, 128, TOT], F32, kind="ExternalInput").ap()
    prm_d = nc.dram_tensor("prm", [128, NPRM], F32, kind="ExternalInput").ap()
    cf_d = nc.dram_tensor("cf", [128, 640], F32, kind="ExternalInput").ap()
    cb_d = nc.dram_tensor("cb", [128, NCB], BF16, kind="ExternalInput").ap()
    out_d = nc.dram_tensor("out", [S, D], F32, kind="ExternalOutput").ap()
    dbg = {}
    if debug:
        for nm, shp in (("d_hT", [128, 8 * S]), ("d_o", [128, 4 * S]), ("d_s", [128, 4 * S]),
                        ("d_xmix", [128, 8 * S]), ("d_xffn", [128, 8 * S]), ("d_q", [128, 8 * S]),
                        ("d_kc", [128, 256]), ("d_vc", [128, 194]), ("d_imp", [128, 16 * 64]),
                        ("d_sel", [128, 16 * 64]), ("d_zc", [128, 4 * S])):
            dbg[nm] = nc.dram_tensor(nm, shp, F32, kind="ExternalOutput").ap()

    es = ExitStack()
    P = Prog(nc)

    def sb(name, shape, dt):
        return es.enter_context(nc.sbuf_tensor(name, shape, dt))

    XT_B = 0
    R1_B = 65536
    R3_B = R1_B + 32768
    R2_B = R3_B + 33024
    WST_B = R2_B + 32768
    WBF_B = WST_B + 16384
    ARENA_B = WBF_B + 8192
    arena = sb("arena", [128, ARENA_B // 4], F32)

    arena_bf = arena[:].bitcast(BF16)

    def view(off_b, nbytes, dt, pattern=None, **kw):
        if dt == BF16:
            ap = arena_bf[:, off_b // 2:(off_b + nbytes) // 2]
        else:
            ap = arena[:, off_b // 4:(off_b + nbytes) // 4]
        if pattern is not None:
            ap = ap.rearrange(pattern, **kw)
        return ap

    xT = view(XT_B, 65536, F32, "p (c t) -> p c t", c=8)
    hT_A = view(R1_B, 32768, BF16, "p (c t) -> p c t", c=8)
    oT = view(R1_B, 16384, BF16, "p (c t) -> p c t", c=4)
    o_tok = view(R1_B + 16384, 8192, F32, "p (a d) -> p a d", a=4)
    pT = [view(R1_B + 24576 + 1024 * i, 1024, BF16) for i in range(4)]
    sT = view(R1_B + 16384, 16384, BF16, "p (c t) -> p c t", c=4)
    qA = view(R2_B, 32768, BF16, "p (h t) -> p h t", h=8)
    hT_C = view(R2_B, 32768, BF16, "p (c t) -> p c t", c=8)
    kA_sel = view(R3_B, 8192, BF16, "p (g t) -> p g t", g=2)
    kA_win = view(R3_B + 8192, 8192, BF16, "p (g t) -> p g t", g=2)
    kcT = view(R3_B + 16384, 8192, BF16, "p (k t) -> p k t", k=2)
    vtok = view(R3_B + 24576, 8320, BF16, "p (t b g d) -> p t b g d", t=16, b=2, g=2)
    ZW = 30 + S
    zT = view(R3_B, 4 * ZW * 2, BF16, "p (c t) -> p c t", c=4)
    zc_R3 = view(R3_B + 16640, 8192, F32, "p (c t) -> p c t", c=4)
    lnt = view(R3_B + 16640 + 8192, 6144, F32, "p (a t) -> p a t", a=3)
    mT = view(R3_B, 32768, BF16, "p (c t) -> p c t", c=8)
    gT = view(R1_B, 45056, BF16, "p (c t) -> p c t", c=22)
    EB = R1_B + 45056
    esil = [view(EB + p_ * 2048, 2048, F32) for p_ in range(2)]
    eacc = [view(EB + 4096 + p_ * 2048, 2048, F32) for p_ in range(2)]
    eub = [view(EB + 8192 + p_ * 1056, 1028, BF16) for p_ in range(2)]
    edg = [view(EB + 8192 + 2112 + p_ * 768, 768, BF16, "p (k n) -> p k n", k=3) for p_ in range(2)]
    wst = [view(WST_B + 8192 * i, 8192, F32) for i in range(2)]
    wbf = [view(WBF_B + 4096 * i, 4096, BF16) for i in range(2)]

    identf = sb("identf", [128, 128], F32)
    SMA = sb("sma", [128, 2688], F32)

    SMA_bf = SMA[:].bitcast(BF16)

    def smv(off_b, nbytes, dt, pattern=None, **kw):
        if dt == BF16:
            ap = SMA_bf[:, off_b // 2:(off_b + nbytes) // 2]
        else:
            ap = SMA[:, off_b // 4:(off_b + nbytes) // 4]
        if pattern is not None:
            ap = ap.rearrange(pattern, **kw)
        return ap

    bonus = smv(0, 2048, F32, "p (a b) -> p a b", a=16)
    prm = sb("prm_sb", [128, NPRM], F32)
    cbs = sb("cbs", [128, CB_CMPNEG], BF16)
    identb = cbs[:, CB_IDENT:CB_IDENT + 128]
    onesb = cbs[:, CB_ONES:CB_ONES + 128]
    causb = cbs[:, CB_CAUS:CB_CAUS + 128]
    edgeb = cbs[:, CB_EDGE:CB_EDGE + 128]
    cmpneg = smv(2048, 4096, BF16)
    vcA = sb("vcA", [128, 2, 97], BF16)
    kA_cmp = sb("kA_cmp", [128, 2, 128], BF16)
    gates = smv(6144, 1536, F32, "p (t n) -> p t n", t=16)
    impacc = smv(7680, 1024, F32, "p (a g k) -> p a g k", a=4, g=2)
    impf = sb("impf", [128, 32], F32)
    scr32 = sb("scr32", [128, 32], F32)
    m8 = sb("m8", [128, 8], F32)
    Maug2 = sb("Maug2", [128, 2, 4, 32], BF16)
    r4 = sb("r4", [128, 4], F32)
    gr4 = sb("gr4", [128, 4], F32)
    rstd = sb("rstd", [128, 512], F32)
    sqb = [sb("sqb%d" % i, [128, 512], BF16) for i in range(2)]
    peT = sb("peT", [128, 64], BF16)
    cbias = sb("cbias", [128, 2], F32)
    hu = sb("hu", [128, 128], F32)
    ht1 = sb("ht1", [128, 128], F32)
    ht2 = sb("ht2", [128, 128], F32)
    hidb = sb("hidb", [128, 128], BF16)
    w2s = sb("w2s", [128, 128], BF16)
    ecrA = sb("ecrA", [128, 2], F32)
    ecrB = sb("ecrB", [128, 2], F32)
    ubuf = [smv(2056 * i, 2056, F32) for i in range(2)]
    facc = [smv(4112 + 2048 * i, 2048, F32) for i in range(2)]
    fsil = smv(8208, 2048, F32)
    ytok = [view(WST_B + 4096 * i, 4096, F32) for i in range(2)]

    banks = [es.enter_context(nc.psum_tensor("bank%d" % i, [128, 512], F32)) for i in range(8)]

    def bk(i):
        return ("ps", i)

    P.dma("sp", "c0", lambda e: e.dma_start(out=identf[:], in_=cf_d[:, 0:128]), writes=["identf"])
    P.dma("sp", "c2", lambda e: e.dma_start(out=prm[:], in_=prm_d[:, :]), writes=["prm"])
    P.dma("sp", "c3", lambda e: e.dma_start(out=cbs[:], in_=cb_d[:, 0:CB_CMPNEG]), writes=["cbs"])
    P.dma("sp", "c4", lambda e: e.dma_start(out=vcA[:, 0, 64:97], in_=cb_d[:, CB_MAP:CB_MAP + 33]), writes=["vcA_c"])
    P.dma("sp", "c4", lambda e: e.dma_start(out=vcA[:, 1, 64:97], in_=cb_d[:, CB_MAP:CB_MAP + 33]), writes=["vcA_c"])

    P.op("pool", lambda e: e.memset(vcA[:, :, 0:64], 0.0), writes=[("vcA", 0), ("vcA", 1)])

    xin = [view(R1_B + 4096 * i, 4096, F32) for i in range(2)]
    for tt in range(16):
        b = tt % 2
        P.dma("sp", "xin%d" % b, lambda e, tt=tt, b=b: e.dma_start(out=xin[b], in_=x_d[tt * 128:(tt + 1) * 128, :]),
              writes=[("xin", b)])
        for half in range(2):
            pb = (tt * 2 + half) % 4
            for j in range(4):
                c = half * 4 + j
                P.op("pe", lambda e, b=b, c=c, j=j, pb=pb: e.transpose(out=banks[pb][:, j * 128:(j + 1) * 128], in_=xin[b][:, c * 128:(c + 1) * 128], identity=identf[:]),
                     reads=[("xin", b), "identf"], writes=[bk(pb)], inc=(j == 3))
            eng = "act" if half == 0 else "dve"
            if eng == "act":
                P.op("act", lambda e, tt=tt, half=half, pb=pb: e.activation(out=xT[:, half * 4:half * 4 + 4, tt * 128:(tt + 1) * 128],
                                                                          in_=banks[pb][:].rearrange("p (j t) -> p j t", j=4), func=AF.Copy),
                     writes=[bk(pb)] + [("xT", half * 4 + j, tt // 4) for j in range(4)])
            else:
                P.op("dve", lambda e, tt=tt, half=half, pb=pb: e.tensor_copy(out=xT[:, half * 4:half * 4 + 4, tt * 128:(tt + 1) * 128],
                                                                           in_=banks[pb][:].rearrange("p (j t) -> p j t", j=4)),
                     writes=[bk(pb)] + [("xT", half * 4 + j, tt // 4) for j in range(4)])
    P.barrier()

    wctr = [0]

    wst.append(view(EB + 12288, 8192, F32))
    assert 8192 + 2112 + 1536 <= 12288
    wbf.append(smv(0, 4096, BF16))

    def wload(l, name, shape_pat=None, cast="pool", slot=None, **kw):
        off, e = UNITS[name]
        if slot is None:
            slot = wctr[0] % 2
            wctr[0] += 1
        P.dma("sp", "w%d" % slot, lambda en: en.dma_start(out=wst[slot][:, 0:e], in_=wpk_d[l, :, off:off + e]),
              writes=[("wst", slot)])
        if cast == "act":
            P.op("act", lambda en: en.activation(out=wbf[slot][:, 0:e], in_=wst[slot][:, 0:e], func=AF.Copy),
                 reads=[("wst", slot)], writes=[("wbf", slot)])
        else:
            P.op(cast, lambda en: en.tensor_copy(out=wbf[slot][:, 0:e], in_=wst[slot][:, 0:e]),
                 reads=[("wst", slot)], writes=[("wbf", slot)])
        ap = wbf[slot][:, 0:e]
        if shape_pat is not None:
            ap = ap.rearrange(shape_pat, **kw)
        return ap, ("wbf", slot)

    def wdma(l, name, stage):
        off, e = UNITS[name]
        P.dma("sp", "w%d" % stage, lambda en: en.dma_start(out=wst[stage][:, 0:e], in_=wpk_d[l, :, off:off + e]), writes=[("wst", stage)])

    def wcast(name, stage, bf, bfcell, cast):
        off, e = UNITS[name]
        if cast == "act":
            P.op("act", lambda en: en.activation(out=bf[:, 0:e], in_=wst[stage][:, 0:e], func=AF.Copy), reads=[("wst", stage)], writes=[bfcell])
        else:
            P.op(cast, lambda en: en.tensor_copy(out=bf[:, 0:e], in_=wst[stage][:, 0:e]), reads=[("wst", stage)], writes=[bfcell])

    class WStream:
        def __init__(self, l, specs, nslots=2, dist=1):
            self.l, self.specs, self.n, self.dist = l, specs, nslots, dist
            self.nxt = 0
            self.aps = {}

        def get(self, i):
            hi = min(len(self.specs) - 1, i + self.dist)
            while self.nxt <= hi:
                name, pat, kw, cast = self.specs[self.nxt]
                self.aps[self.nxt] = wload(self.l, name, pat, cast=cast, slot=self.nxt % self.n, **kw)
                self.nxt += 1
            return self.aps.pop(i)

    def rmsnorm(hT, gcol, hname, sbank=6):
        for tg in range(4):
            ts = slice(tg * 512, (tg + 1) * 512)
            for c in range(8):
                q = c % 2
                P.op("act", lambda e, c=c, q=q: e.activation(out=sqb[q][:], in_=xT[:, c, ts], func=AF.Square),
                     reads=[("xT", c, tg)], writes=[("sqb", q)])
                P.op("pe", lambda e, c=c, q=q: e.matmul(out=banks[sbank][:], lhsT=onesb, rhs=sqb[q][:], start=(c == 0), stop=(c == 7)),
                     reads=[("sqb", q), "cbs"], writes=[bk(sbank)], inc=True)
            P.op("act", lambda e: e.activation(out=rstd[:], in_=banks[sbank][:], func=AF.Sqrt, bias=EPS, scale=1.0 / D),
                 writes=[bk(sbank), "rstd"])
            P.op("dve", lambda e: e.reciprocal(out=rstd[:], in_=rstd[:]), writes=["rstd"])
            for c in range(8):
                P.op("dve", lambda e, c=c: e.scalar_tensor_tensor(out=hT[:, c, ts], in0=xT[:, c, ts], scalar=prm[:, gcol + c:gcol + c + 1],
                                                                  in1=rstd[:], op0=ALU.mult, op1=ALU.mult),
                     reads=[("xT", c, tg), "rstd", "prm"], writes=[(hname, c, tg)])

    def dump(name, ap_src, cols, cells):
        if debug and name in dbg:
            P.barrier()
            P.dma("sp", "dbg_" + name, lambda e: e.dma_start(out=dbg[name][:, 0:cols], in_=ap_src), reads=cells)
            P.barrier()

    slopes = [2.0 ** (-(h + 1)) for h in range(NH)]

    for l in range(nlayers):
        pb0 = l * NPRM_L
        rmsnorm(hT_A, pb0 + 0, "hT")
        if debug and l == 0:
            pass
        P.dma("sp", "c1", lambda e: e.dma_start(out=bonus.rearrange("p a b -> p (a b)"), in_=cf_d[:, 128:640]), writes=["bonus"])
        P.dma("sp", "c5", lambda e: e.dma_start(out=cmpneg, in_=cb_d[:, CB_CMPNEG:CB_CMPNEG + 2048]), writes=["cmpneg"])
        P.dma("sp", "aq", lambda e: e.dma_start(out=qA[0:64, :, :].rearrange("p h t -> p (h t)"), in_=cb_d[0:64, CB_QAUG:CB_QAUG + 8 * S]), writes=["qA_aug"])
        for g in range(2):
            P.dma("sp", "ak", lambda e, g=g: e.dma_start(out=kA_sel[0:64, g, :], in_=cb_d[0:64, CB_KSEL:CB_KSEL + S]), writes=[("kA_aug", 0, g)])
            P.dma("sp", "ak", lambda e, g=g: e.dma_start(out=kA_win[0:64, g, :], in_=cb_d[0:64, CB_KWIN:CB_KWIN + S]), writes=[("kA_aug", 1, g)])
            P.dma("sp", "ak", lambda e, g=g: e.dma_start(out=kA_cmp[0:64, g, :], in_=cb_d[0:64, CB_KCMP:CB_KCMP + 128]), writes=[("kA_aug", 2, g)])
        P.op("pool", lambda e: e.memset(kA_cmp[64:128, :, :], 0.0), writes=["kA_cmp_d"])
        P.op("pool", lambda e: e.memset(vtok[:, :, :, :, 64:65], 1.0), writes=["vtok_ones"])

        pbr = 0
        wsA = WStream(l, [(("FM", u), "p (k n) -> p k n", dict(k=8), "pool") for u in range(4)]
                      + [("TOKA", "p (k n) -> p k n", dict(k=8), "pool"), ("TOKB", "p (k n) -> p k n", dict(k=8), "pool")])
        for u in range(4):
            w, wc = wsA.get(u)
            for tg in range(4):
                ts = slice(tg * 512, (tg + 1) * 512)
                for ci in range(2):
                    ch = u * 2 + ci
                    pb = pbr % 4
                    pbr += 1
                    for k in range(8):
                        P.op("pe", lambda e, k=k, ci=ci, pb=pb, w=w: e.matmul(out=banks[pb][:], lhsT=w[:, k, ci * 128:(ci + 1) * 128], rhs=hT_A[:, k, ts],
                                                                             start=(k == 0), stop=(k == 7)),
                             reads=[wc, ("hT", k, tg)], writes=[bk(pb)], inc=(k == 7))
                    if ch < 4:
                        P.op("dve", lambda e, ch=ch, pb=pb: e.tensor_copy(out=qA[64:128, 2 * ch, ts], in_=banks[pb][0:64, :]),
                             writes=[bk(pb), ("qA", 2 * ch, tg)])
                        P.op("act", lambda e, ch=ch, pb=pb: e.activation(out=qA[64:128, 2 * ch + 1, ts], in_=banks[pb][64:128, :], func=AF.Copy),
                             writes=[bk(pb), ("qA", 2 * ch + 1, tg)])
                    elif ch in (4, 5):
                        P.op("act", lambda e, ch=ch, pb=pb: e.activation(out=kcT[:, ch - 4, ts], in_=banks[pb][:], func=AF.Copy),
                             writes=[bk(pb), ("kcT", ch - 4, tg)])
                    else:
                        dst = kA_sel if ch == 6 else kA_win
                        nm = "kS" if ch == 6 else "kW"
                        P.op("dve", lambda e, dst=dst, pb=pb: e.tensor_copy(out=dst[64:128, 0, ts], in_=banks[pb][0:64, :]),
                             writes=[bk(pb), (nm, 0, tg)])
                        P.op("act", lambda e, dst=dst, pb=pb: e.activation(out=dst[64:128, 1, ts], in_=banks[pb][64:128, :], func=AF.Copy),
                             writes=[bk(pb), (nm, 1, tg)])
        w, wc = wsA.get(4)
        for tt in range(16):
            pb = 4 + tt % 2
            for k in range(8):
                P.op("pe", lambda e, k=k, tt=tt, pb=pb, w=w: e.matmul(out=banks[pb][:, 0:256], lhsT=hT_A[:, k, tt * 128:(tt + 1) * 128], rhs=w[:, k, :],
                                                                     start=(k == 0), stop=(k == 7)),
                     reads=[wc, ("hT", k, tt // 4)], writes=[bk(pb)], inc=(k == 7))
            eng = "act" if tt % 2 == 0 else "dve"
            if eng == "act":
                P.op("act", lambda e, tt=tt, pb=pb: e.activation(out=vtok[:, tt, :, :, 0:64].rearrange("p b g d -> p (b g) d"),
                                                               in_=banks[pb][:, 0:256].rearrange("p (a d) -> p a d", d=64), func=AF.Copy),
                     reads=["vtok_ones"], writes=[bk(pb), ("vtok", tt)])
            else:
                P.op("dve", lambda e, tt=tt, pb=pb: e.tensor_copy(out=vtok[:, tt, :, :, 0:64].rearrange("p b g d -> p (b g) d"),
                                                                in_=banks[pb][:, 0:256].rearrange("p (a d) -> p a d", d=64)),
                     reads=["vtok_ones"], writes=[bk(pb), ("vtok", tt)])
        w, wc = wsA.get(5)
        for tt in range(16):
            pb = 4 + tt % 2
            for k in range(8):
                P.op("pe", lambda e, k=k, tt=tt, pb=pb, w=w: e.matmul(out=banks[pb][:, 0:24], lhsT=hT_A[:, k, tt * 128:(tt + 1) * 128], rhs=w[:, k, :],
                                                                     start=(k == 0), stop=(k == 7)),
                     reads=[wc, ("hT", k, tt // 4)], writes=[bk(pb)], inc=(k == 7))
            P.op("act", lambda e, tt=tt, pb=pb: e.activation(out=gates[:, tt, :], in_=banks[pb][:, 0:24], func=AF.Sigmoid),
                 writes=[bk(pb), ("gates", tt)])

        P.op("dve", lambda e: e.tensor_copy(out=peT[:], in_=prm[:, pb0 + 328:pb0 + 392]), reads=["prm"], writes=["peT"])
        w2t, w2tc = wload(l, "W2")
        P.op("dve", lambda e: e.tensor_copy(out=w2s[:], in_=w2t), reads=[w2tc], writes=["w2s"])
        w2, w2c = w2s, "w2s"
        kc16 = kcT[:, :, :].rearrange("p k (n s) -> p k n s", s=16)
        for kv in range(2):
            wh = []
            for half in range(2):
                wh.append(wload(l, ("CW", kv, half), "p (a c) -> p a c", cast=("act" if half == 0 else "pool"), a=16))
            for li in range(32):
                w, wc = wh[li // 16]
                P.op("pe", lambda e, li=li, w=w, kv=kv: e.matmul(out=banks[7][:, kv:kv + 1], lhsT=w[0:64, li % 16, :], rhs=peT[0:64, kv * 32 + li:kv * 32 + li + 1],
                                                                start=(li == 0), stop=(li == 31)),
                     reads=[wc, "peT"], writes=[bk(7)], inc=(li == 31))
            P.op("dve", lambda e, kv=kv: e.tensor_copy(out=cbias[:, kv:kv + 1], in_=banks[7][:, kv:kv + 1]), writes=[bk(7), ("cbias", kv)])
            for g in range(2):
                rows = slice(g * 64, (g + 1) * 64)
                for li in range(32):
                    w, wc = wh[li // 16]
                    if li < 16:
                        rhs = kc16[rows, kv, 0:127, li]
                    else:
                        rhs = kc16[rows, kv, 1:128, li - 16]
                    P.op("pe", lambda e, li=li, w=w, rhs=rhs, rows=rows: e.matmul(out=banks[5][:, 0:127], lhsT=w[rows, li % 16, :], rhs=rhs,
                                                                                 start=(li == 0), stop=(li == 31)),
                         reads=[wc] + [("kcT", kv, tg) for tg in range(4)], writes=[bk(5)], inc=(li == 31))
                P.op("act", lambda e, kv=kv: e.activation(out=hu[:, 0:127], in_=banks[5][:, 0:127], func=AF.Identity, bias=cbias[:, kv:kv + 1], scale=1.0),
                     reads=[("cbias", kv)], writes=[bk(5), "hu"])
                P.op("dve", lambda e: e.tensor_tensor(out=ht1[:, 0:127], in0=hu[:, 0:127], in1=hu[:, 0:127], op=ALU.mult), reads=["hu"], writes=["ht1"])
                P.op("dve", lambda e: e.tensor_scalar(out=ht1[:, 0:127], in0=ht1[:, 0:127], scalar1=0.044715, scalar2=1.0, op0=ALU.mult, op1=ALU.add), writes=["ht1"])
                P.op("dve", lambda e: e.tensor_tensor(out=ht2[:, 0:127], in0=ht1[:, 0:127], in1=hu[:, 0:127], op=ALU.mult), reads=["ht1", "hu"], writes=["ht2"])
                P.op("act", lambda e: e.activation(out=ht1[:, 0:127], in_=ht2[:, 0:127], func=AF.Sigmoid, scale=1.5957691216057308), reads=["ht2"], writes=["ht1"])
                P.op("dve", lambda e: e.tensor_tensor(out=hidb[:, 0:127], in0=ht1[:, 0:127], in1=hu[:, 0:127], op=ALU.mult), reads=["ht1", "hu"], writes=["hidb"])
                if kv == 0:
                    P.op("pe", lambda e, w2=w2: e.matmul(out=banks[6][0:64, 0:127], lhsT=w2[:, 0:64], rhs=hidb[:, 0:127], start=True, stop=True),
                         reads=[w2c, "hidb"], writes=[bk(6)], inc=True)
                    P.op("dve", lambda e, g=g: e.tensor_copy(out=kA_cmp[64:128, g, 0:127], in_=banks[6][0:64, 0:127]),
                         reads=["kA_cmp_d"], writes=[bk(6), ("kA_cmp", g)])
                else:
                    P.op("pe", lambda e, w2=w2: e.matmul(out=banks[6][0:127, 0:64], lhsT=hidb[:, 0:127], rhs=w2[:, 64:128], start=True, stop=True),
                         reads=[w2c, "hidb"], writes=[bk(6)], inc=True)
                    P.op("dve", lambda e, g=g: e.tensor_copy(out=vcA[0:127, g, 0:64], in_=banks[6][0:127, 0:64]),
                         writes=[bk(6), ("vcA", g)])
        P.barrier()
        if debug and l == 0:
            P.op("dve", lambda e: e.tensor_copy(out=ytok[0][:, 0:256], in_=kA_cmp[:].rearrange("p g n -> p (g n)")), writes=["ytok0"])
            dump("d_kc", ytok[0][:, 0:256], 256, ["ytok0"])
            P.op("dve", lambda e: e.tensor_copy(out=ytok[0][:, 0:194], in_=vcA[:].rearrange("p g n -> p (g n)")), writes=["ytok0"])
            dump("d_vc", ytok[0][:, 0:194], 194, ["ytok0"])
            for i4 in range(4):
                P.op("dve", lambda e, i4=i4: e.tensor_copy(out=ytok[0], in_=kcT[:, i4 // 2, (i4 % 2) * 1024:(i4 % 2 + 1) * 1024]), writes=["ytok0"])
                P.barrier()
                P.dma("sp", "dbg_q", lambda e, i4=i4: e.dma_start(out=dbg["d_q"][:, i4 * 1024:(i4 + 1) * 1024], in_=ytok[0]), reads=["ytok0"])
                P.barrier()
            P.op("dve", lambda e: e.tensor_copy(out=ytok[0][:, 0:128], in_=hu[:]), writes=["ytok0"])
            P.op("dve", lambda e: e.tensor_copy(out=ytok[0][:, 128:130], in_=cbias[:]), writes=["ytok0"])
            P.op("dve", lambda e: e.tensor_copy(out=ytok[0][:, 130:258], in_=hidb[:]), writes=["ytok0"])
            P.op("dve", lambda e: e.tensor_copy(out=ytok[0][:, 258:322], in_=peT[:]), writes=["ytok0"])
            dump("d_hT", ytok[0][:, 0:322], 322, ["ytok0"])
        if stop_after == "A":
            break

        sctr = [0]
        pctr = [0]
        otr = [0]
        bctr = [0]
        LOOK = 3

        def emit_qk(it):
            blk = it["blk"]
            h, g, br, tg = blk["h"], blk["g"], blk["br"], blk["tg"]
            kt, c0, c1 = it["kt"], it["c0"], it["c1"]
            sbk = sctr[0] % 3
            sctr[0] += 1
            if br == 0:
                lhs = kA_cmp[:, g, :]
                kcell = [("kA_cmp", g), ("kA_aug", 2, g)]
            elif br == 1:
                lhs = kA_sel[:, g, kt * 128:(kt + 1) * 128]
                kcell = [("kS", g, kt // 4), ("kA_aug", 0, g)]
            else:
                lhs = kA_win[:, g, kt * 128:(kt + 1) * 128]
                kcell = [("kW", g, kt // 4), ("kA_aug", 1, g)]
            masks = []
            if br == 0:
                masks.append((0, 512, cmpneg[:, tg * 512:(tg + 1) * 512]))
            else:
                dcol = 128 * kt - 512 * tg
                if 0 <= dcol < 512:
                    masks.append((dcol, dcol + 128, causb))
                if br == 2:
                    ecol = 128 * (kt + 4) - 512 * tg
                    if 0 <= ecol < 512:
                        masks.append((ecol, ecol + 128, edgeb))
            qcells = [("qA", h, tg), "qA_aug"] + ([("qsel", h, tg)] if br == 1 else [])
            P.op("pe", lambda e: e.matmul(out=banks[sbk][:, c0:c1], lhsT=lhs, rhs=qA[:, h, tg * 512 + c0:tg * 512 + c1], start=True, stop=(len(masks) == 0)),
                 reads=kcell + qcells, writes=[bk(sbk)], inc=(len(masks) == 0))
            for mi, (m0, m1, mrhs) in enumerate(masks):
                last = (mi == len(masks) - 1)
                P.op("pe", lambda e: e.matmul(out=banks[sbk][:, m0:m1], lhsT=identb, rhs=mrhs, start=False, stop=last, skip_group_check=True),
                     reads=["cbs", "cmpneg"], writes=[bk(sbk)], inc=last)
            return sbk

        def emit_exp(it, sbk):
            blk = it["blk"]
            h, br, tg = blk["h"], blk["br"], blk["tg"]
            kt, c0, c1 = it["kt"], it["c0"], it["c1"]
            pi = pctr[0] % 4
            pctr[0] += 1
            if br == 0:
                bias = -slopes[h] * (512 * tg - 31)
            else:
                bias = -slopes[h] * (512 * tg - 128 * kt)
            P.op("act", lambda e: e.activation(out=pT[pi][:, c0:c1], in_=banks[sbk][:, c0:c1], func=AF.Exp, bias=float(bias), scale=0.125),
                 writes=[bk(sbk), ("pT", pi)])
            return pi

        def emit_pv(it, pi):
            blk = it["blk"]
            g, br, ab, nca = blk["g"], blk["br"], blk["ab"], blk["ncols"]
            kt, c0, c1 = it["kt"], it["c0"], it["c1"]
            ttls = list(range(c0 // 128, c1 // 128))
            for ttl in ttls:
                if br == 0:
                    rhs = vcA[:, g, :]
                    vcell = [("vcA", g), "vcA_c"]
                else:
                    rhs = vtok[:, kt, br - 1, g, :]
                    vcell = [("vtok", kt)]
                st = blk["first"]
                blk["first"] = False
                P.op("pe", lambda e: e.matmul(out=banks[ab][:, ttl * nca:(ttl + 1) * nca], lhsT=pT[pi][:, ttl * 128:(ttl + 1) * 128], rhs=rhs,
                                              start=st, stop=True, skip_group_check=True),
                     reads=[("pT", pi)] + vcell, writes=[bk(ab)], inc=(ttl == ttls[-1]))

        ntmp = view(WST_B, 1024, F32, "p (a d) -> p a d", a=4)

        def normalize(blk):
            h, g, br, tg, ab, ncols_acc = blk["h"], blk["g"], blk["br"], blk["tg"], blk["ab"], blk["ncols"]
            accv = banks[ab][:, 0:4 * ncols_acc].rearrange("p (a n) -> p a n", a=4)
            P.op("dve", lambda e: e.tensor_scalar(out=r4[:], in0=accv[:, :, 64], scalar1=1e-30, scalar2=None, op0=ALU.max), writes=[bk(ab), "r4"])
            P.op("dve", lambda e: e.reciprocal(out=r4[:], in_=r4[:]), writes=["r4"])
            P.op("dve", lambda e: e.tensor_tensor(out=gr4[:], in0=r4[:], in1=gates[:, tg * 4:tg * 4 + 4, h * 3 + br], op=ALU.mult),
                 reads=["r4"] + [("gates", tg * 4 + a) for a in range(4)], writes=["gr4"])
            grb = gr4[:, :].unsqueeze(2).to_broadcast([128, 4, 64])
            osl = o_tok[:, :, h * 64:(h + 1) * 64]
            ocells = [("o_tok", a) for a in range(4)]
            if br == 0:
                P.op("dve", lambda e: e.tensor_tensor(out=osl, in0=accv[:, :, 0:64], in1=grb, op=ALU.mult), reads=["gr4"], writes=[bk(ab)] + ocells)
                rb = r4[:, :].unsqueeze(2).to_broadcast([128, 4, 32])
                isl = impacc[:, :, g, :]
                icells = [("impacc", a, g) for a in range(4)]
                if h % 4 == 0:
                    P.op("dve", lambda e: e.tensor_tensor(out=isl, in0=accv[:, :, 65:97], in1=rb, op=ALU.mult), reads=["r4"], writes=[bk(ab)] + icells)
                else:
                    P.op("dve", lambda e: e.tensor_tensor(out=ntmp[:, :, 0:32], in0=accv[:, :, 65:97], in1=rb, op=ALU.mult), reads=["r4"], writes=[bk(ab), "ntmp"])
                    P.op("dve", lambda e: e.tensor_tensor(out=isl, in0=isl, in1=ntmp[:, :, 0:32], op=ALU.add), reads=["ntmp"], writes=icells)
            else:
                P.op("dve", lambda e: e.tensor_tensor(out=ntmp, in0=accv[:, :, 0:64], in1=grb, op=ALU.mult), reads=["gr4"], writes=[bk(ab), "ntmp"])
                P.op("dve", lambda e: e.tensor_tensor(out=osl, in0=osl, in1=ntmp, op=ALU.add), reads=["ntmp"], writes=ocells)

        def selection_topk(tg):
            for g in range(2):
                for a in range(4):
                    tt = tg * 4 + a
                    P.op("dve", lambda e: e.tensor_tensor(out=impf[:], in0=impacc[:, a, g, :], in1=bonus[:, tt, :], op=ALU.add),
                         reads=[("impacc", a, g), "bonus"], writes=["impf"])
                    P.op("dve", lambda e: e.max(out=m8[:], in_=impf[:]), reads=["impf"], writes=["m8"])
                    P.op("dve", lambda e: e.match_replace(out=scr32[:], in_to_replace=m8[:], in_values=impf[:], imm_value=-3.0e38), reads=["impf", "m8"], writes=["scr32"])
                    P.op("dve", lambda e: e.max(out=m8[:], in_=scr32[:]), reads=["scr32"], writes=["m8"])
                    P.op("dve", lambda e: e.tensor_scalar(out=Maug2[:, g, a, :], in0=impf[:], scalar1=m8[:, 7:8], scalar2=-BIG, op0=ALU.is_lt, op1=ALU.mult),
                         reads=["impf", "m8"], writes=[("Maug", g, a)])

        def selection_apply(tg):
            psT = banks[5][:].bitcast(BF16)
            for g in range(2):
                for a in range(4):
                    P.op("pe", lambda e: e.transpose(out=psT[0:32, a * 128:(a + 1) * 128], in_=Maug2[:, g, a, :], identity=identb),
                         reads=[("Maug", g, a), "cbs"], writes=[bk(5)], inc=True)
                for j in range(4):
                    hh = g * 4 + j
                    if j % 2 == 0:
                        P.op("act", lambda e: e.activation(out=qA[0:32, hh, tg * 512:(tg + 1) * 512], in_=psT[0:32, 0:512], func=AF.Copy),
                             reads=["qA_aug"], writes=[bk(5), ("qsel", hh, tg)])
                    else:
                        P.op("dve", lambda e: e.tensor_copy(out=qA[0:32, hh, tg * 512:(tg + 1) * 512], in_=psT[0:32, 0:512]),
                             reads=["qA_aug"], writes=[bk(5), ("qsel", hh, tg)])

        for tg in range(4):
            items = []

            def add_block(h, br, kts, ncols):
                blk = dict(h=h, g=h // 4, br=br, tg=tg, ncols=ncols, ab=3 + bctr[0] % 3, first=True)
                bctr[0] += 1
                for i, kt in enumerate(kts):
                    if br == 0:
                        c0, c1 = 0, 512
                    elif br == 1:
                        c0, c1 = max(0, 128 * kt - 512 * tg), 512
                    else:
                        c0 = max(0, 128 * kt - 512 * tg)
                        c1 = min(512, 128 * (kt + 5) - 512 * tg)
                    items.append(dict(blk=blk, kt=kt, c0=c0, c1=c1, last=(i == len(kts) - 1)))

            for h in range(NH):
                add_block(h, 0, [0], 97)
            for h in range(NH):
                add_block(h, 2, list(range(max(0, 4 * tg - 4), 4 * tg + 4)), 65)
            for h in range(NH):
                add_block(h, 1, list(range(0, 4 * tg + 4)), 65)

            pend = []

            def drain_one():
                it, pi = pend.pop(0)
                emit_pv(it, pi)
                if it["last"]:
                    normalize(it["blk"])
                    if it["blk"]["br"] == 0 and it["blk"]["h"] == NH - 1:
                        selection_topk(tg)

            applied = False
            for it in items:
                if it["blk"]["br"] == 1 and not applied:
                    selection_apply(tg)
                    applied = True
                sbk = emit_qk(it)
                pi = emit_exp(it, sbk)
                pend.append((it, pi))
                if len(pend) > LOOK:
                    drain_one()
            while pend:
                drain_one()
            for a in range(4):
                ob = 6 + otr[0] % 2
                otr[0] += 1
                for c in range(4):
                    P.op("pe", lambda e, a=a, c=c, ob=ob: e.transpose(out=banks[ob][:, c * 128:(c + 1) * 128], in_=o_tok[:, a, c * 128:(c + 1) * 128], identity=identf[:]),
                         reads=[("o_tok", a), "identf"], writes=[bk(ob)], inc=(c == 3))
                t0 = tg * 512 + a * 128
                P.op("act", lambda e, ob=ob, t0=t0: e.activation(out=oT[:, :, t0:t0 + 128], in_=banks[ob][:].rearrange("p (c t) -> p c t", c=4), func=AF.Copy),
                     writes=[bk(ob), ("oT", tg)])
        P.barrier()
        if debug and l == 0:
            for c in range(4):
                for hf in range(2):
                    P.op("dve", lambda e, c=c, hf=hf: e.tensor_copy(out=ytok[0], in_=oT[:, c, hf * 1024:(hf + 1) * 1024]), writes=["ytok0"])
                    P.barrier()
                    P.dma("sp", "dbg_o", lambda e, c=c, hf=hf: e.dma_start(out=dbg["d_o"][:, c * S + hf * 1024:c * S + (hf + 1) * 1024], in_=ytok[0]), reads=["ytok0"])
                    P.barrier()
        if stop_after == "B":
            break

        rmsnorm(hT_C, pb0 + 0, "hT")
        P.op("pool", lambda e: e.memset(zT[:, :, 0:30], 0.0), writes=["zpad"])
        wsC = WStream(l, [(("UC", c), "p (k n) -> p k n", dict(k=8), "pool") for c in range(4)])
        for c in range(4):
            w, wc = wsC.get(c)
            for tg in range(4):
                ts = slice(tg * 512, (tg + 1) * 512)
                for half in range(2):
                    pb = half
                    for k in range(8):
                        P.op("pe", lambda e, k=k, half=half, pb=pb, w=w: e.matmul(out=banks[pb][:], lhsT=w[:, k, half * 128:(half + 1) * 128], rhs=hT_C[:, k, ts],
                                                                                 start=(k == 0), stop=(k == 7)),
                             reads=[wc, ("hT", k, tg)], writes=[bk(pb)], inc=(k == 7))
                P.op("act", lambda e: e.activation(out=fsil, in_=banks[1][:], func=AF.Sigmoid), writes=[bk(1), "fsil"])
                P.op("dve", lambda e, c=c, tg=tg: e.tensor_tensor(out=zT[:, c, 30 + tg * 512:30 + (tg + 1) * 512], in0=banks[0][:], in1=fsil, op=ALU.mult),
                     reads=["fsil"], writes=[bk(0), ("zT", c, tg)])
        cw0 = pb0 + 16
        dctr = [0]
        dgb = [smv(0, 7936, BF16, "p (k n) -> p k n", k=31), view(WST_B, 7936, BF16, "p (k n) -> p k n", k=31)]
        P.barrier()
        zc2 = [zc_R3, view(WST_B + 8192, 8192, F32, "p (c t) -> p c t", c=4)]
        def conv_tg(tg):
            zc = zc2[tg % 2]
            zx = [("wst", 1)] if tg % 2 == 1 else []
            for c in range(4):
                zr = [("zT", c, t_) for t_ in range(max(0, tg - 1), tg + 1)] + ["zpad"]
                di = dctr[0] % 2
                dctr[0] += 1
                dg = dgb[di]
                dx = [("wst", 0)] if di == 1 else []
                for k in range(31):
                    P.op("dve", lambda e: e.tensor_scalar(out=dg[:, k, :], in0=identb, scalar1=prm[:, cw0 + c * 31 + k:cw0 + c * 31 + k + 1], scalar2=None, op0=ALU.mult),
                         reads=["cbs", "prm"], writes=[("dg", di, k)])
                cb_ = 4 + dctr[0] % 4
                for k in range(31):
                    P.op("pe", lambda e: e.matmul(out=banks[cb_][:], lhsT=dg[:, k, :], rhs=zT[:, c, tg * 512 + k:tg * 512 + k + 512], start=(k == 0), stop=(k == 30)),
                         reads=zr + [("dg", di, k)], writes=[bk(cb_)], inc=(k == 30))
                P.op("act", lambda e: e.activation(out=zc[:, c, :], in_=banks[cb_][:], func=AF.Identity, bias=prm[:, pb0 + 140 + c:pb0 + 141 + c], scale=1.0),
                     reads=["prm"], writes=[bk(cb_), ("zc", tg % 2, c)])

        def ln_tg(tg):
            zc = zc2[tg % 2]
            if debug and l == 0:
                for c in range(4):
                    P.barrier()
                    P.dma("sp", "dbg_zc", lambda e, c=c, tg=tg: e.dma_start(out=dbg["d_zc"][:, c * S + tg * 512:c * S + (tg + 1) * 512], in_=zc[:, c, :]), reads=[("zc", tg % 2, c)])
                    P.barrier()
            for c in range(4):
                q = c % 2
                P.op("act", lambda e, c=c, q=q: e.activation(out=sqb[q][:], in_=zc[:, c, :], func=AF.Copy), reads=[("zc", tg % 2, c)], writes=[("sqb", q)])
                P.op("pe", lambda e, c=c, q=q: e.matmul(out=banks[2][:], lhsT=onesb, rhs=sqb[q][:], start=(c == 0), stop=(c == 3)),
                     reads=[("sqb", q), "cbs"], writes=[bk(2)], inc=True)
            for c in range(4):
                q = c % 2
                P.op("act", lambda e, c=c, q=q: e.activation(out=sqb[q][:], in_=zc[:, c, :], func=AF.Square), reads=[("zc", tg % 2, c)], writes=[("sqb", q)])
                P.op("pe", lambda e, c=c, q=q: e.matmul(out=banks[3][:], lhsT=onesb, rhs=sqb[q][:], start=(c == 0), stop=(c == 3)),
                     reads=[("sqb", q), "cbs"], writes=[bk(3)], inc=True)
            mean = lnt[:, 0, :]
            var = lnt[:, 1, :]
            tmp = lnt[:, 2, :]
            P.op("act", lambda e: e.activation(out=mean, in_=banks[2][:], func=AF.Copy, scale=1.0 / 512), writes=[bk(2), "mean"])
            P.op("dve", lambda e: e.tensor_tensor(out=tmp, in0=mean, in1=mean, op=ALU.mult), reads=["mean"], writes=["lntmp"])
            P.op("dve", lambda e: e.scalar_tensor_tensor(out=var, in0=banks[3][:], scalar=1.0 / 512, in1=tmp, op0=ALU.mult, op1=ALU.subtract),
                 reads=["lntmp"], writes=[bk(3), "var"])
            P.op("dve", lambda e: e.tensor_scalar(out=var, in0=var, scalar1=0.0, scalar2=None, op0=ALU.max), writes=["var"])
            P.op("act", lambda e: e.activation(out=var, in_=var, func=AF.Sqrt, bias=EPS, scale=1.0), writes=["var"])
            P.op("dve", lambda e: e.reciprocal(out=var, in_=var), writes=["var"])
            for c in range(4):
                P.op("dve", lambda e, c=c: e.tensor_tensor(out=zc[:, c, :], in0=zc[:, c, :], in1=mean, op=ALU.subtract), reads=["mean"], writes=[("zc", tg % 2, c)])
                P.op("dve", lambda e, c=c: e.tensor_tensor(out=zc[:, c, :], in0=zc[:, c, :], in1=var, op=ALU.mult), reads=["var"], writes=[("zc", tg % 2, c)])
                P.op("act", lambda e, c=c, tg=tg: e.activation(out=sT[:, c, tg * 512:(tg + 1) * 512], in_=zc[:, c, :], func=AF.Silu,
                                                             scale=prm[:, pb0 + 144 + c:pb0 + 145 + c], bias=prm[:, pb0 + 148 + c:pb0 + 149 + c]),
                     reads=[("zc", tg % 2, c), "prm"], writes=[("sT", c, tg)])

        conv_tg(0)
        for tg in range(4):
            if tg + 1 < 4:
                conv_tg(tg + 1)
            ln_tg(tg)
        P.barrier()
        if debug and l == 0:
            for c in range(4):
                for hf in range(2):
                    P.op("dve", lambda e, c=c, hf=hf: e.tensor_copy(out=ytok[0], in_=sT[:, c, hf * 1024:(hf + 1) * 1024]), writes=["ytok0"])
                    P.barrier()
                    P.dma("sp", "dbg_s", lambda e, c=c, hf=hf: e.dma_start(out=dbg["d_s"][:, c * S + hf * 1024:c * S + (hf + 1) * 1024], in_=ytok[0]), reads=["ytok0"])
                    P.barrier()
        if stop_after == "C":
            break

        facc2 = [[facc[0], facc[1]], [smv(0, 2048, F32), smv(2048, 2048, F32)]]
        dit = 0
        wsm = smv(8208, 2048, BF16)
        wdma(l, ("D1a", 0), 0)
        wcast(("D1a", 0), 0, wbf[0], ("wbf", 0), "act")
        wdma(l, ("D1b", 0), 1)
        wcast(("D1b", 0), 1, wsm, "wsm", "pool")
        for oc in range(8):
            wa = wbf[oc % 2][:, 0:2048].rearrange("p (k n) -> p k n", k=8)
            wac = ("wbf", oc % 2)
            wb_ = wsm[:, 0:1024].rearrange("p (k b n) -> p k b n", k=4, b=2)
            wbc = "wsm"
            if oc + 1 < 8:
                wdma(l, ("D1a", oc + 1), 0)
                wcast(("D1a", oc + 1), 0, wbf[(oc + 1) % 2], ("wbf", (oc + 1) % 2), "pool")
                wdma(l, ("D1b", oc + 1), 1)
            for tg in range(4):
                ts = slice(tg * 512, (tg + 1) * 512)
                bb = 4 * (dit % 2)
                fa = facc2[dit % 2]
                fc = ("faccD", dit % 2)
                dit += 1
                for half in range(2):
                    for k in range(8):
                        P.op("pe", lambda e: e.matmul(out=banks[bb + half][:], lhsT=wa[:, k, half * 128:(half + 1) * 128], rhs=hT_C[:, k, ts], start=(k == 0), stop=(k == 7)),
                             reads=[wac, ("hT", k, tg)], writes=[bk(bb + half)], inc=(k == 7))
                for k in range(4):
                    P.op("pe", lambda e: e.matmul(out=banks[bb + 2][:], lhsT=wb_[:, k, 0, :], rhs=oT[:, k, ts], start=(k == 0), stop=(k == 3)),
                         reads=[wbc, ("oT", tg)], writes=[bk(bb + 2)], inc=(k == 3))
                for k in range(4):
                    P.op("pe", lambda e: e.matmul(out=banks[bb + 3][:], lhsT=wb_[:, k, 1, :], rhs=sT[:, k, ts], start=(k == 0), stop=(k == 3)),
                         reads=[wbc, ("sT", k, tg)], writes=[bk(bb + 3)], inc=(k == 3))
                P.op("act", lambda e: e.activation(out=fa[0], in_=banks[bb][:], func=AF.Sigmoid), writes=[bk(bb), fc + (0,)])
                P.op("act", lambda e: e.activation(out=fa[1], in_=banks[bb + 1][:], func=AF.Sigmoid), writes=[bk(bb + 1), fc + (1,)])
                P.op("dve", lambda e: e.tensor_tensor(out=fa[0], in0=banks[bb + 2][:], in1=fa[0], op=ALU.mult), writes=[bk(bb + 2), fc + (0,)])
                P.op("dve", lambda e: e.tensor_tensor(out=fa[1], in0=banks[bb + 3][:], in1=fa[1], op=ALU.mult), writes=[bk(bb + 3), fc + (1,)])
                P.op("pool", lambda e: e.tensor_tensor(out=mT[:, oc, tg * 512:(tg + 1) * 512], in0=fa[0], in1=fa[1], op=ALU.add),
                     reads=[fc + (0,), fc + (1,)], writes=[("mT", oc, tg)])
            if oc + 1 < 8:
                wcast(("D1b", oc + 1), 1, wsm, "wsm", "act")
        wsO = WStream(l, [(("WO", oc), "p (k n) -> p k n", dict(k=8), "pool") for oc in range(8)])
        for oc in range(8):
            w, wc = wsO.get(oc)
            for tg in range(4):
                ts = slice(tg * 512, (tg + 1) * 512)
                pb = 4 + (oc * 4 + tg) % 2
                for k in range(8):
                    P.op("pe", lambda e, k=k, pb=pb, w=w: e.matmul(out=banks[pb][:], lhsT=w[:, k, :], rhs=mT[:, k, ts], start=(k == 0), stop=(k == 7)),
                         reads=[wc, ("mT", k, tg)], writes=[bk(pb)], inc=(k == 7))
                P.op("dve", lambda e, oc=oc, pb=pb: e.tensor_tensor(out=xT[:, oc, ts], in0=banks[pb][:], in1=xT[:, oc, ts], op=ALU.add),
                     writes=[bk(pb), ("xT", oc, tg)])
        P.barrier()
        if debug and l == 0:
            for c in range(8):
                P.dma("sp", "dbg_xm", lambda e, c=c: e.dma_start(out=dbg["d_xmix"][:, c * S:(c + 1) * S], in_=xT[:, c, :]), reads=[("xT", c, t_) for t_ in range(4)])
            P.barrier()
        if stop_after == "D":
            break

        rmsnorm(hT_C, pb0 + 8, "hT")
        fw0 = pb0 + 152
        fb0 = pb0 + 284
        itc = 0
        especs = []
        for th in range(2):
            especs += [(("UP", c), "p (k n) -> p k n", dict(k=8), "act") for c in range(22)]
            for oc in range(8):
                especs += [(("DN", oc, 0), "p (k n) -> p k n", dict(k=11), "pool"), (("DN", oc, 1), "p (k n) -> p k n", dict(k=11), "act")]
        wsE = WStream(l, especs, nslots=3, dist=2)
        epend = [None]
        for th in range(2):
            for c in range(22):
                w, wc = wsE.get(th * 38 + c)
                dg = edg[c % 2]
                cha, chv = c, c + 22
                wa2 = prm[:, fw0 + cha * 3 + 2:fw0 + cha * 3 + 3]
                wa1 = prm[:, fw0 + cha * 3 + 1:fw0 + cha * 3 + 2]
                wa0 = prm[:, fw0 + cha * 3:fw0 + cha * 3 + 1]
                for k in range(3):
                    P.op("dve", lambda e: e.tensor_scalar(out=dg[:, k, :], in0=identb, scalar1=prm[:, fw0 + chv * 3 + k:fw0 + chv * 3 + k + 1], scalar2=None, op0=ALU.mult),
                         reads=["cbs", "prm"], writes=[("edg", c % 2, k)])
                for tq in range(2):
                    tg = th * 2 + tq
                    ts = slice(tg * 512, (tg + 1) * 512)
                    par = itc % 2
                    itc += 1
                    ua, uv, cv = par, 2 + par, 4 + par
                    ac = eacc[par]
                    ub = eub[par]
                    for k in range(8):
                        P.op("pe", lambda e: e.matmul(out=banks[ua][:], lhsT=w[:, k, 0:128], rhs=hT_C[:, k, ts], start=(k == 0), stop=(k == 7)),
                             reads=[wc, ("hT", k, tg)], writes=[bk(ua)], inc=(k == 7))
                    if tq == 0 and tg > 0:
                        for k in range(8):
                            P.op("pe", lambda e: e.matmul(out=banks[6][:, 0:2], lhsT=w[:, k, 0:128], rhs=hT_C[:, k, tg * 512 - 2:tg * 512], start=(k == 0), stop=(k == 7)),
                                 reads=[wc, ("hT", k, tg - 1)], writes=[bk(6)], inc=(k == 7))
                        P.op("dve", lambda e: e.tensor_copy(out=ecrA[:], in_=banks[6][:, 0:2]), writes=[bk(6), "ecrA"])
                    for k in range(8):
                        P.op("pe", lambda e: e.matmul(out=banks[uv][:], lhsT=w[:, k, 128:256], rhs=hT_C[:, k, ts], start=(k == 0), stop=(k == 7)),
                             reads=[wc, ("hT", k, tg)], writes=[bk(uv)], inc=(k == 7))
                    if tg == 0:
                        P.op("dve", lambda e: e.memset(ub[:, 0:2], 0.0), writes=[("eub", par)])
                    elif tq == 0:
                        for k in range(8):
                            P.op("pe", lambda e: e.matmul(out=banks[7][:, 0:2], lhsT=w[:, k, 128:256], rhs=hT_C[:, k, tg * 512 - 2:tg * 512], start=(k == 0), stop=(k == 7)),
                                 reads=[wc, ("hT", k, tg - 1)], writes=[bk(7)], inc=(k == 7))
                        P.op("dve", lambda e: e.tensor_copy(out=ub[:, 0:2], in_=banks[7][:, 0:2]), writes=[bk(7), ("eub", par)])
                    else:
                        P.op("dve", lambda e: e.tensor_copy(out=ub[:, 0:2], in_=eub[1 - par][:, 512:514]), reads=[("eub", 1 - par)], writes=[("eub", par)])
                    P.op("act", lambda e: e.activation(out=ac, in_=banks[ua][:], func=AF.Identity, scale=wa2, bias=prm[:, fb0 + cha:fb0 + cha + 1]),
                         reads=["prm"], writes=[bk(ua), ("eacc", par)])
                    P.op("act", lambda e: e.activation(out=ub[:, 2:514], in_=banks[uv][:], func=AF.Copy), writes=[bk(uv), ("eub", par)])
                    P.op("dve", lambda e: e.scalar_tensor_tensor(out=ac[:, 1:512], in0=banks[ua][:, 0:511], scalar=wa1, in1=ac[:, 1:512], op0=ALU.mult, op1=ALU.add),
                         writes=[bk(ua), ("eacc", par)])
                    crcur = ecrA if tq == 0 else ecrB
                    crn = "ecrA" if tq == 0 else "ecrB"
                    if tq == 0:
                        P.op("dve", lambda e: e.tensor_copy(out=ecrB[:], in_=banks[ua][:, 510:512]), writes=[bk(ua), "ecrB"])
                    P.op("dve", lambda e: e.scalar_tensor_tensor(out=ac[:, 2:512], in0=banks[ua][:, 0:510], scalar=wa0, in1=ac[:, 2:512], op0=ALU.mult, op1=ALU.add),
                         writes=[bk(ua), ("eacc", par)])
                    if tg > 0:
                        P.op("dve", lambda e: e.scalar_tensor_tensor(out=ac[:, 0:1], in0=crcur[:, 1:2], scalar=wa1, in1=ac[:, 0:1], op0=ALU.mult, op1=ALU.add),
                             reads=[crn], writes=[("eacc", par)])
                        P.op("dve", lambda e: e.scalar_tensor_tensor(out=ac[:, 0:2], in0=crcur[:, 0:2], scalar=wa0, in1=ac[:, 0:2], op0=ALU.mult, op1=ALU.add),
                             reads=[crn], writes=[("eacc", par)])
                    P.op("act", lambda e: e.activation(out=esil[par], in_=ac, func=AF.Silu), reads=[("eacc", par)], writes=[("esil", par)])

                    def mk_part2(c=c, tq=tq, par=par, ub=ub, dg=dg, cv=cv, chv=chv):
                        def part2():
                            for k in range(3):
                                P.op("pe", lambda e: e.matmul(out=banks[cv][:], lhsT=dg[:, k, :], rhs=ub[:, k:k + 512], start=(k == 0), stop=(k == 2)),
                                     reads=[("edg", c % 2, k), ("eub", par)], writes=[bk(cv)], inc=(k == 2))
                            P.op("dve", lambda e: e.scalar_tensor_tensor(out=gT[:, c, tq * 512:(tq + 1) * 512], in0=banks[cv][:], scalar=prm[:, fb0 + chv:fb0 + chv + 1],
                                                                         in1=esil[par], op0=ALU.add, op1=ALU.mult),
                                 reads=[("esil", par), "prm"], writes=[bk(cv), ("gT", c, tq)])
                        return part2

                    if epend[0] is not None:
                        epend[0]()
                    epend[0] = mk_part2()
            if epend[0] is not None:
                epend[0]()
                epend[0] = None
            for oc in range(8):
                for hf in range(2):
                    wk, wkc = wsE.get(th * 38 + 22 + oc * 2 + hf)
                    for tq in range(2):
                        pb = 4 + (oc % 2) * 2 + tq
                        for kk in range(11):
                            k = hf * 11 + kk
                            P.op("pe", lambda e: e.matmul(out=banks[pb][:], lhsT=wk[:, kk, :], rhs=gT[:, k, tq * 512:(tq + 1) * 512],
                                                          start=(k == 0), stop=(k == 21)),
                                 reads=[wkc, ("gT", k, tq)], writes=[bk(pb)], inc=(kk == 10))
                for tq in range(2):
                    tg = th * 2 + tq
                    pb = 4 + (oc % 2) * 2 + tq
                    P.op("dve", lambda e: e.tensor_tensor(out=xT[:, oc, tg * 512:(tg + 1) * 512], in0=banks[pb][:], in1=xT[:, oc, tg * 512:(tg + 1) * 512], op=ALU.add),
                         writes=[bk(pb), ("xT", oc, tg)])
        P.barrier()
        if debug and l == 0:
            for c in range(8):
                P.dma("sp", "dbg_xf", lambda e, c=c: e.dma_start(out=dbg["d_xffn"][:, c * S:(c + 1) * S], in_=xT[:, c, :]), reads=[("xT", c, t_) for t_ in range(4)])
            P.barrier()

    yT = view(R2_B, 32768, F32, "p (c t) -> p c t", c=8)
    for th in range(2):
        for tq in range(2):
            tg = th * 2 + tq
            ts = slice(tg * 512, (tg + 1) * 512)
            for c in range(8):
                q = c % 2
                P.op("act", lambda e, c=c, q=q: e.activation(out=sqb[q][:], in_=xT[:, c, ts], func=AF.Square), reads=[("xT", c, tg)], writes=[("sqb", q)])
                P.op("pe", lambda e, c=c, q=q: e.matmul(out=banks[6][:], lhsT=onesb, rhs=sqb[q][:], start=(c == 0), stop=(c == 7)),
                     reads=[("sqb", q), "cbs"], writes=[bk(6)], inc=True)
            P.op("act", lambda e: e.activation(out=rstd[:], in_=banks[6][:], func=AF.Sqrt, bias=EPS, scale=1.0 / D), writes=[bk(6), "rstd"])
            P.op("dve", lambda e: e.reciprocal(out=rstd[:], in_=rstd[:]), writes=["rstd"])
            for c in range(8):
                P.op("dve", lambda e, c=c, tq=tq: e.scalar_tensor_tensor(out=yT[:, c, tq * 512:(tq + 1) * 512], in0=xT[:, c, ts], scalar=prm[:, 2 * NPRM_L + c:2 * NPRM_L + c + 1],
                                                                        in1=rstd[:], op0=ALU.mult, op1=ALU.mult),
                     reads=[("xT", c, tg), "rstd", "prm"], writes=[("yT", c, tq)])
        for tl in range(8):
            tt = th * 8 + tl
            yb = tt % 2
            for half in range(2):
                pb = (tt * 2 + half) % 4
                for j in range(4):
                    c = half * 4 + j
                    P.op("pe", lambda e, c=c, j=j, pb=pb, tl=tl: e.transpose(out=banks[pb][:, j * 128:(j + 1) * 128], in_=yT[:, c, tl * 128:(tl + 1) * 128], identity=identf[:]),
                         reads=[("yT", c, tl // 4), "identf"], writes=[bk(pb)], inc=(j == 3))
                if half == 0:
                    P.op("act", lambda e, yb=yb, pb=pb: e.activation(out=ytok[yb][:, 0:512], in_=banks[pb][:], func=AF.Copy), writes=[bk(pb), ("ytok", yb, 0)])
                else:
                    P.op("dve", lambda e, yb=yb, pb=pb: e.tensor_copy(out=ytok[yb][:, 512:1024], in_=banks[pb][:]), writes=[bk(pb), ("ytok", yb, 1)])
            P.dma("sp", "out%d" % yb, lambda e, tt=tt, yb=yb: e.dma_start(out=out_d[tt * 128:(tt + 1) * 128, :], in_=ytok[yb]),
                  reads=[("ytok", yb, 0), ("ytok", yb, 1)])
        P.barrier()
    P.barrier()
    P.emit()
    es.close()
    return nc


def _pack_weights(inp):
    wpk = np.zeros((L, 128, TOT), np.float32)
    for l in range(L):
        w_in = inp["w_in"][l]
        def kc(wcols):
            n = wcols.shape[1]
            return wcols.reshape(8, 128, n).transpose(1, 0, 2).reshape(128, 8 * n)
        def put(name, arr):
            off, e = UNITS[name]
            assert arr.shape == (128, e), (name, arr.shape, e)
            wpk[l, :, off:off + e] = arr
        fm = np.concatenate([w_in[:, 0:512], w_in[:, 512:640], w_in[:, 640:768], w_in[:, 768:896], w_in[:, 1024:1152]], axis=1)
        for u in range(4):
            put(("FM", u), kc(fm[:, u * 256:(u + 1) * 256]))
        put("TOKA", kc(np.concatenate([w_in[:, 896:1024], w_in[:, 1152:1280]], axis=1)))
        put("TOKB", kc(w_in[:, 1280:1304]))
        uc = w_in[:, 1304:2328]
        for c in range(4):
            put(("UC", c), kc(np.concatenate([uc[:, c * 128:(c + 1) * 128], uc[:, 512 + c * 128:512 + (c + 1) * 128]], axis=1)))
        gm = w_in[:, 2328:4376]
        wab = inp["w_attn_br"][l]
        wcb = inp["w_conv_br"][l]
        for oc in range(8):
            put(("D1a", oc), kc(np.concatenate([gm[:, oc * 128:(oc + 1) * 128], gm[:, 1024 + oc * 128:1024 + (oc + 1) * 128]], axis=1)))
            a = wab[:, oc * 128:(oc + 1) * 128].reshape(4, 128, 128).transpose(1, 0, 2)
            b = wcb[:, oc * 128:(oc + 1) * 128].reshape(4, 128, 128).transpose(1, 0, 2)
            put(("D1b", oc), np.stack([a, b], axis=2).reshape(128, 1024))
            put(("WO", oc), kc(inp["w_out"][l][:, oc * 128:(oc + 1) * 128]))
        wup = inp["ffn_w_up"][l]
        for c in range(22):
            put(("UP", c), kc(np.concatenate([wup[:, c * 128:(c + 1) * 128], wup[:, FFN + c * 128:FFN + (c + 1) * 128]], axis=1)))
        wdn = inp["ffn_w_down"][l]
        for oc in range(8):
            blk = wdn[:, oc * 128:(oc + 1) * 128].reshape(22, 128, 128).transpose(1, 0, 2)
            put(("DN", oc, 0), blk[:, 0:11].reshape(128, 1408))
            put(("DN", oc, 1), blk[:, 11:22].reshape(128, 1408))
        for kv, nm in enumerate(("cmp_k_w1", "cmp_v_w1")):
            w1 = inp[nm][l].reshape(32, 64, 128).transpose(1, 0, 2)
            w1 = np.concatenate([w1, w1], axis=0)
            put(("CW", kv, 0), w1[:, 0:16].reshape(128, 2048))
            put(("CW", kv, 1), w1[:, 16:32].reshape(128, 2048))
        put("W2", np.concatenate([inp["cmp_k_w2"][l], inp["cmp_v_w2"][l]], axis=1))
    return wpk


def _pack_params(inp):
    prm = np.zeros((128, NPRM), np.float32)
    for l in range(L):
        b = l * NPRM_L
        prm[:, b:b + 8] = inp["norm1_g"][l].reshape(8, 128).T
        prm[:, b + 8:b + 16] = inp["norm2_g"][l].reshape(8, 128).T
        prm[:, b + 16:b + 140] = inp["conv_dw_w"][l].reshape(31, 4, 128).transpose(2, 1, 0).reshape(128, 124)
        prm[:, b + 140:b + 144] = inp["conv_dw_b"][l].reshape(4, 128).T
        prm[:, b + 144:b + 148] = inp["conv_ln_g"][l].reshape(4, 128).T
        prm[:, b + 148:b + 152] = inp["conv_ln_b"][l].reshape(4, 128).T
        prm[:, b + 152:b + 284] = inp["ffn_dw_w"][l].reshape(3, 44, 128).transpose(2, 1, 0).reshape(128, 132)
        prm[:, b + 284:b + 328] = inp["ffn_dw_b"][l].reshape(44, 128).T
        pk = inp["cmp_pe_k"][l].T
        pv = inp["cmp_pe_v"][l].T
        prm[:, b + 328:b + 360] = np.concatenate([pk, pk], axis=0)
        prm[:, b + 360:b + 392] = np.concatenate([pv, pv], axis=0)
    prm[:, 2 * NPRM_L:2 * NPRM_L + 8] = inp["final_g"].reshape(8, 128).T
    return prm


def _constants():
    cf = np.zeros((128, 640), np.float32)
    cf[:, 0:128] = np.eye(128, dtype=np.float32)
    bonus = np.zeros((128, 16, 32), np.float32)
    for tt in range(16):
        t = tt * 128 + np.arange(128)
        cur = t // 64
        blk = np.arange(32)[None, :]
        forced = (blk == 0) | (blk == cur[:, None]) | (blk == cur[:, None] - 1)
        bonus[:, tt, :] = np.where(blk <= cur[:, None], 1.0e4 * forced, -1.0e30)
    cf[:, 128:640] = bonus.reshape(128, 512)

    cb = np.zeros((128, NCB), np.float32)
    cb[:, CB_IDENT:CB_IDENT + 128] = np.eye(128)
    cb[:, CB_ONES:CB_ONES + 128] = 1.0
    n = np.arange(128)[:, None]
    t = np.arange(128)[None, :]
    cb[:, CB_CAUS:CB_CAUS + 128] = np.where(t >= n, 0.0, -BIG)
    cb[:, CB_EDGE:CB_EDGE + 128] = np.where(t < n, 0.0, -BIG)
    tt_ = np.arange(S)[None, :]
    cb[:, CB_CMPNEG:CB_CMPNEG + S] = np.where((tt_ >= 16 * n + 31) & (n < 127), 0.0, -BIG)
    nn = np.arange(S)
    ks = np.zeros((64, S), np.float32)
    ks[nn // 64, nn] = 1.0
    ks[32, :] = nn % 128
    ks[33, :] = 1.0
    ks[34, :] = 1.0
    cb[0:64, CB_KSEL:CB_KSEL + S] = ks
    kw = ks.copy()
    kw[0:32, :] = 0.0
    cb[0:64, CB_KWIN:CB_KWIN + S] = kw
    kcm = np.zeros((64, 128), np.float32)
    kcm[32, :] = 16.0 * np.arange(128)
    kcm[33, :] = 1.0
    kcm[34, :] = 1.0
    cb[0:64, CB_KCMP:CB_KCMP + 128] = kcm
    ncmp = 127
    cs = np.arange(ncmp) * 16
    ce = cs + 31
    ss = np.arange(32) * 64
    se = ss + 63
    ov = np.minimum(ce[:, None], se[None, :]) - np.maximum(cs[:, None], ss[None, :]) + 1
    mp = np.clip(ov, 0, None).astype(np.float32) / 32.0
    cb[:, CB_MAP] = 1.0
    cb[0:127, CB_MAP + 1:CB_MAP + 33] = mp
    qa = np.zeros((64, 8, S), np.float32)
    tm = np.arange(S) % 512
    for h in range(8):
        sl = 2.0 ** (-(h + 1))
        qa[32, h, :] = 8.0 * sl
        qa[33, h, :] = -8.0 * sl * 128.0 * (tm // 128)
        qa[34, h, :] = -8.0 * sl * (tm % 128)
    cb[0:64, CB_QAUG:CB_QAUG + 8 * S] = qa.reshape(64, 8 * S)
    return cf, cb.astype(ml_dtypes.bfloat16)


_CACHE = {}


def kernel(**inputs):
    inp = {k: np.asarray(v) for k, v in inputs.items()}
    x = np.ascontiguousarray(inp["x"], dtype=np.float32)
    wpk = _pack_weights(inp)
    prm = _pack_params(inp)
    cf, cb = _constants()
    if "nc" not in _CACHE:
        _CACHE["nc"] = build_program()
    nc = _CACHE["nc"]
    in_maps = [{"x": x[b], "wpk": wpk, "prm": prm, "cf": cf, "cb": cb} for b in range(8)]
    res = run_bass_kernel_spmd(nc, in_maps, core_ids=list(range(8)))
    return np.stack([np.asarray(r["out"], dtype=np.float32) for r in res.results], axis=0)
```
